# Optimizing a Trainium2 kernel written in Bass

```python
import math
import jax, jax.numpy as jnp
from jax import lax
import numpy as np

D_MODEL = 1024
BATCH = 8
SEQ = 2048
DEPTH = 2

GRID_W = 64
NA_HEADS = 8
NA_HEAD_DIM = 64
NA_WIDTH = NA_HEADS * NA_HEAD_DIM
NA_KH = 8
NA_KW = 16
NA_QB = 16
NA_KB = NA_QB + NA_KW
NA_NCB = GRID_W // NA_QB
HY_WIDTH = D_MODEL - NA_WIDTH
D_MIX = NA_WIDTH + HY_WIDTH
SHORT_CONV = 3
FILTER_EMB = 33
FILTER_ORDER = 64
DECAY_TARGET = 1e-2
FAST_DECAY_PCT = 0.3
SLOW_DECAY_PCT = 1.5
MAX_DECAY = math.log(DECAY_TARGET) / FAST_DECAY_PCT
MIN_DECAY = math.log(DECAY_TARGET) / SLOW_DECAY_PCT
D_FF = 2816
PLE_DIM = 256
EPS = 1e-6

kernel_name = 'hybrid_na_hyena_macaron_block'


def _rms(x):
    xf = x.astype(jnp.float32)
    return (xf * lax.rsqrt(jnp.mean(xf * xf, axis=-1, keepdims=True) + EPS)).astype(x.dtype)


def rmsnorm(x, g):
    return _rms(x) * g


def swiglu(h, w_gate, w_up, w_down):
    return (jax.nn.silu(h @ w_gate) * (h @ w_up)) @ w_down


def neighbourhood_attention(q, k, v, rpb):
    B, L, H, dh = q.shape
    rows = L // GRID_W
    kh = min(NA_KH, rows)
    nk = kh * NA_KB
    r = jnp.arange(rows)
    row_start = jnp.clip(r - kh // 2, 0, rows - kh)
    key_rows = row_start[:, None] + jnp.arange(kh)[None, :]
    qcol = jnp.arange(NA_NCB)[:, None] * NA_QB + jnp.arange(NA_QB)[None, :]
    band_c0 = jnp.clip(jnp.arange(NA_NCB) * NA_QB - NA_KW // 2, 0, GRID_W - NA_KB)
    key_cols = band_c0[:, None] + jnp.arange(NA_KB)[None, :]
    key_idx = (key_rows[:, None, :, None] * GRID_W + key_cols[None, :, None, :]).reshape(rows, NA_NCB, nk)
    kg = k[:, key_idx]
    vg = v[:, key_idx]
    qb = q.reshape(B, rows, NA_NCB, NA_QB, H, dh)
    s = jnp.einsum('brjqhd,brjkhd->bhrjqk', qb, kg).astype(jnp.float32) * (dh ** -0.5)
    krow = jnp.broadcast_to(key_rows[:, :, None], (rows, kh, NA_KB)).reshape(rows, nk)
    kcol = jnp.broadcast_to(key_cols[:, None, :], (NA_NCB, kh, NA_KB)).reshape(NA_NCB, nk)
    dr = krow - r[:, None]
    dc = kcol[:, None, :] - qcol[:, :, None]
    qcs = jnp.clip(qcol - NA_KW // 2, 0, GRID_W - NA_KW)
    mask = (kcol[:, None, :] >= qcs[..., None]) & (kcol[:, None, :] < qcs[..., None] + NA_KW)
    dr_idx = (dr + NA_KH - 1)[:, None, None, :]
    dc_idx = jnp.clip(dc + NA_KW - 1, 0, 2 * NA_KW - 2)[None]
    bias = rpb[:, dr_idx, dc_idx].astype(jnp.float32)
    s = jnp.where(mask[None, None, None], s + bias[None], -jnp.inf)
    prob = jax.nn.softmax(s, axis=-1).astype(vg.dtype)
    o = jnp.einsum('bhrjqk,brjkhd->brjqhd', prob, vg)
    return o.reshape(B, L, H * dh)


def short_conv(u, w, b):
    up = jnp.pad(u, ((0, 0), (1, 1), (0, 0)))
    return up[:, :-2] * w[0] + up[:, 1:-1] * w[1] + up[:, 2:] * w[2] + b


def implicit_filter(L, w_f1, b_f1, w_f2, b_f2, w_f3, b_f3, w_f4, freq):
    bands = (FILTER_EMB - 1) // 2
    t = jnp.linspace(0.0, 1.0, L, dtype=jnp.float32)[:, None]
    w = 2.0 * math.pi * jnp.arange(L, dtype=jnp.float32)[:, None] / L
    f = jnp.linspace(1e-4, bands - 1, bands, dtype=jnp.float32)[None, :]
    z = jnp.concatenate([t, jnp.cos(f * w), -jnp.sin(f * w)], axis=-1)
    h = jnp.sin(freq * (z @ w_f1 + b_f1))
    h = jnp.sin(freq * (h @ w_f2 + b_f2))
    h = jnp.sin(freq * (h @ w_f3 + b_f3))
    h = (h @ w_f4).astype(jnp.float32)
    deltas = jnp.abs(jnp.linspace(MIN_DECAY, MAX_DECAY, HY_WIDTH, dtype=jnp.float32))
    decay = jnp.exp(-t * deltas[None, :])
    h = h.reshape(L, 2, HY_WIDTH) * decay[:, None, :]
    h_fwd, h_bwd = h[:, 0], h[:, 1]
    k = jnp.concatenate([h_fwd[:1] + h_bwd[:1], h_fwd[1:], jnp.zeros((1, HY_WIDTH), jnp.float32),
                         h_bwd[1:][::-1]], axis=0)
    return k / jnp.sum(jnp.abs(k), axis=0, keepdims=True)


def long_conv(u, k, bias):
    L = u.shape[1]
    n = 2 * L
    u_f = jnp.fft.rfft(u.astype(jnp.float32), n=n, axis=1)
    k_f = jnp.fft.rfft(k, n=n, axis=0)
    y = jnp.fft.irfft(u_f * k_f[None], n=n, axis=1)[:, :L]
    return y.astype(u.dtype) + u * bias


def setup_inputs(seed: int = 0) -> dict:
    key = jax.random.key(seed)
    ks = iter(jax.random.split(key, 40))

    def nrm(shape, scale):
        return jax.random.normal(next(ks), shape, jnp.float32) * scale

    def gain(shape):
        return 1.0 + nrm(shape, 0.02)

    D = DEPTH
    return {
        'x': nrm((BATCH, SEQ, D_MODEL), 1.0),
        'p': nrm((DEPTH, BATCH, SEQ, PLE_DIM), 1.0),
        'g_ffa': gain((D, D_MODEL)),
        'w_ffa_gate': nrm((D, D_MODEL, D_FF), D_MODEL ** -0.5),
        'w_ffa_up': nrm((D, D_MODEL, D_FF), D_MODEL ** -0.5),
        'w_ffa_down': nrm((D, D_FF, D_MODEL), D_FF ** -0.5),
        'g_mix': gain((D, D_MODEL)),
        'w_in': nrm((D, D_MODEL, 3 * NA_WIDTH + 3 * HY_WIDTH), D_MODEL ** -0.5),
        'na_rpb': nrm((D, NA_HEADS, 2 * NA_KH - 1, 2 * NA_KW - 1), 0.02),
        'w_sc': nrm((D, SHORT_CONV, 3 * HY_WIDTH), SHORT_CONV ** -0.5),
        'b_sc': nrm((D, 3 * HY_WIDTH), 0.02),
        'w_f1': nrm((D, FILTER_EMB, FILTER_ORDER), FILTER_EMB ** -0.5),
        'b_f1': nrm((D, FILTER_ORDER), 0.02),
        'w_f2': nrm((D, FILTER_ORDER, FILTER_ORDER), FILTER_ORDER ** -0.5),
        'b_f2': nrm((D, FILTER_ORDER), 0.02),
        'w_f3': nrm((D, FILTER_ORDER, FILTER_ORDER), FILTER_ORDER ** -0.5),
        'b_f3': nrm((D, FILTER_ORDER), 0.02),
        'w_f4': nrm((D, FILTER_ORDER, 2 * HY_WIDTH), FILTER_ORDER ** -0.5),
        'filt_freq': gain((D, FILTER_ORDER)),
        'hy_bias': nrm((D, HY_WIDTH), 1.0),
        'g_out': gain((D, D_MIX)),
        'w_out': nrm((D, D_MIX, D_MODEL), D_MIX ** -0.5),
        'g_ffb': gain((D, D_MODEL)),
        'w_ffb_gate': nrm((D, D_MODEL, D_FF), D_MODEL ** -0.5),
        'w_ffb_up': nrm((D, D_MODEL, D_FF), D_MODEL ** -0.5),
        'w_ffb_down': nrm((D, D_FF, D_MODEL), D_FF ** -0.5),
        'g_ple': gain((D, D_MODEL)),
        'w_ple_gate': nrm((D, D_MODEL, D_MODEL), D_MODEL ** -0.5),
        'w_ple_proj': nrm((D, PLE_DIM, D_MODEL), PLE_DIM ** -0.5),
        'g_final': gain((D_MODEL,)),
    }


def reference(x, p, g_ffa, w_ffa_gate, w_ffa_up, w_ffa_down, g_mix, w_in, na_rpb, w_sc, b_sc,
              w_f1, b_f1, w_f2, b_f2, w_f3, b_f3, w_f4, filt_freq, hy_bias, g_out, w_out,
              g_ffb, w_ffb_gate, w_ffb_up, w_ffb_down, g_ple, w_ple_gate, w_ple_proj, g_final):
    B, L, _ = x.shape
    for i in range(DEPTH):
        x = x + 0.5 * swiglu(rmsnorm(x, g_ffa[i]), w_ffa_gate[i], w_ffa_up[i], w_ffa_down[i])
        u = rmsnorm(x, g_mix[i]) @ w_in[i]
        q, k, v, hy = jnp.split(u, [NA_WIDTH, 2 * NA_WIDTH, 3 * NA_WIDTH], axis=-1)
        hs = (B, L, NA_HEADS, NA_HEAD_DIM)
        y_na = neighbourhood_attention(q.reshape(hs), k.reshape(hs), v.reshape(hs), na_rpb[i])
        hy = short_conv(hy, w_sc[i], b_sc[i])
        x0, x1, hv = jnp.split(hy, 3, axis=-1)
        filt = implicit_filter(L, w_f1[i], b_f1[i], w_f2[i], b_f2[i], w_f3[i], b_f3[i], w_f4[i], filt_freq[i])
        y_hy = x0 * long_conv(hv * x1, filt, hy_bias[i])
        y = jnp.concatenate([_rms(y_na), _rms(y_hy)], axis=-1) * g_out[i]
        x = x + y @ w_out[i]
        x = x + 0.5 * swiglu(rmsnorm(x, g_ffb[i]), w_ffb_gate[i], w_ffb_up[i], w_ffb_down[i])
        x = x + jax.nn.sigmoid(rmsnorm(x, g_ple[i]) @ w_ple_gate[i]) * (p[i] @ w_ple_proj[i])
    return rmsnorm(x, g_final)
```

```python
import math
import os
import numpy as np
_STOP = int(os.environ.get('FILT_STOP', '9'))
_VAR = os.environ.get('FILT_VAR', '')
_NAS = int(os.environ.get('NA_STOP', '9'))
import concourse.bass as bass
import concourse.mybir as mybir
from concourse.bass_utils import run_bass_kernel_spmd

F32 = mybir.dt.float32
BF16 = mybir.dt.bfloat16
AF = mybir.ActivationFunctionType
ALU = mybir.AluOpType

L = 2048
D = 1024
DFF = 2816
NF = DFF // 128
NCH = D // 128
TT = 512
NT = L // TT
DEPTH = 2
EPS = 1e-6
NHEAD = 8
HY = 512
NFFT = 4096
FILT_ORDER = 64
FILT_EMB = 33
PLE = 256
GRID_W = 64
NROWS = L // GRID_W
NDR = 14

SB_BASE = 20480
OFF_X = 0
OFF_CONST = 65536
CONST_BYTES = 6144
OFF_WRING = OFF_CONST + CONST_BYTES
WSLOT = 4096
NWSLOT = 6
OFF_R = OFF_WRING + WSLOT * NWSLOT
SB_TOP = 225280
R_BYTES = SB_TOP - SB_BASE - OFF_R
OFF_YNHY = 90304

CP = {}
_cp_n = 0


def _cp(name, n):
    global _cp_n
    CP[name] = _cp_n
    _cp_n += n


for _l in range(DEPTH):
    for _nm in ("g_ffa", "g_mix", "g_ffb", "g_ple", "g_out"):
        _cp(f"{_nm}{_l}", 8)
    _cp(f"wsc{_l}", 36)
    _cp(f"bsc{_l}", 12)
    for _nm in ("b_f1", "b_f2", "b_f3", "freq"):
        _cp(f"{_nm}{_l}", 1)
_cp("g_final", 8)
CP_N = _cp_n


def _chunked(v):
    return np.ascontiguousarray(v.reshape(-1, 128).T)


def build_cpack(inp):
    cp = np.zeros((128, CP_N), np.float32)
    for l in range(DEPTH):
        for nm in ("g_ffa", "g_mix", "g_ffb", "g_ple", "g_out"):
            cp[:, CP[f"{nm}{l}"]:CP[f"{nm}{l}"] + 8] = _chunked(inp[nm][l])
        w = inp["w_sc"][l]
        for k in range(3):
            cp[:, CP[f"wsc{l}"] + k * 12: CP[f"wsc{l}"] + (k + 1) * 12] = _chunked(w[k])
        cp[:, CP[f"bsc{l}"]:CP[f"bsc{l}"] + 12] = _chunked(inp["b_sc"][l])
        for nm, src in (("b_f1", "b_f1"), ("b_f2", "b_f2"), ("b_f3", "b_f3"), ("freq", "filt_freq")):
            cp[:64, CP[f"{nm}{l}"]] = inp[src][l]
    cp[:, CP["g_final"]:CP["g_final"] + 8] = _chunked(inp["g_final"])
    return cp


_CONST_CACHE = {}


def host_constants():
    if _CONST_CACHE:
        return _CONST_CACHE
    t = np.arange(L, dtype=np.float64)
    f = np.arange(L, dtype=np.float64)
    ang = 2.0 * np.pi * ((t[:, None] * f[None, :]) % NFFT) / NFFT
    C = np.cos(ang)
    S = np.sin(ang)
    S[:, 0] = np.where((np.arange(L) % 2) == 0, 1.0, -1.0)
    FW = np.stack([C, S], axis=0)
    FW = FW.reshape(2, 16, 128, 16, 128)
    FW = np.ascontiguousarray(FW.transpose(3, 2, 1, 0, 4)).astype(np.float32)
    wgt = np.full(L, 2.0 / NFFT)
    wgt[0] = 1.0 / NFFT
    GC = (C * wgt[None, :]).T
    GS = (S * wgt[None, :]).T
    GI = np.stack([GC, GS], axis=0).reshape(2, 16, 128, 4, 512)
    GI = np.ascontiguousarray(GI.transpose(3, 1, 2, 0, 4)).astype(np.float32)
    bands = (FILT_EMB - 1) // 2
    tt = np.linspace(0.0, 1.0, L, dtype=np.float32)[:, None]
    w = (2.0 * np.float32(math.pi) * np.arange(L, dtype=np.float32)[:, None] / np.float32(L)).astype(np.float32)
    fb = np.linspace(1e-4, bands - 1, bands, dtype=np.float32)[None, :]
    z = np.concatenate([tt, np.cos(fb * w), -np.sin(fb * w)], axis=-1).astype(np.float32)
    zT = np.ascontiguousarray(z.T)
    max_decay = math.log(1e-2) / 0.3
    min_decay = math.log(1e-2) / 1.5
    deltas = np.abs(np.linspace(min_decay, max_decay, HY, dtype=np.float32))
    decay = np.exp(-tt * deltas[None, :]).astype(np.float32)
    decay = np.ascontiguousarray(decay.reshape(16, 128, HY).transpose(1, 0, 2))
    _CONST_CACHE.update(FW=FW, GI=GI, zT=zT, decay=decay)
    return _CONST_CACHE


def build_bias_table(na_rpb):
    NEG = np.float32(-30000.0)
    p = np.arange(128)
    kcol = p % 64
    krow = p // 64
    q = np.arange(64)
    qcs = np.clip(q - 8, 0, 48)
    mask = (kcol[:, None] >= qcs[None, :]) & (kcol[:, None] < qcs[None, :] + 16)
    dc = np.clip(kcol[:, None] - q[None, :] + 15, 0, 30)
    tab = np.full((DEPTH, NHEAD, 128, NDR, 64), NEG, np.float32)
    for d in range(NDR):
        dr = (d - 7) + krow
        ok = (dr >= -7) & (dr <= 7)
        dri = np.clip(dr + 7, 0, 14)
        vals = na_rpb[:, :, dri[:, None], dc]
        m2 = mask & ok[:, None]
        tab[:, :, :, d, :] = np.where(m2[None, None], vals, NEG)
    return tab


class Sem:
    def __init__(self, nc, name):
        self.h = nc.alloc_semaphore(name)
        self.cnt = 0
        self.name = name


class Eng:
    def __init__(self, nc, e, name, is_pe=False):
        self.e = e
        self.sem = Sem(nc, "s_" + name)
        self.seen = {}
        self.is_pe = is_pe
        self.name = name
        self.last = None

    def wait(self, toks):
        best = {}
        for t in toks:
            if t is None:
                continue
            sem, val = t
            if self.is_pe and sem is self.sem:
                continue
            if self.seen.get(sem, 0) >= val:
                continue
            if best.get(sem, (None, 0))[1] < val:
                best[sem] = (sem, val)
        for sem, val in best.values():
            self.e.wait_ge(sem.h, val)
            self.seen[sem] = val

    def mark(self, ins):
        self.sem.cnt += 1
        ins.then_inc(self.sem.h, 1)
        self.last = (self.sem, self.sem.cnt)
        return self.last


class B:
    def __init__(self):
        nc = bass.Bass("TRN2", target_bir_lowering=False)
        self.nc = nc
        self.pe = Eng(nc, nc.tensor, "pe", is_pe=True)
        self.act = Eng(nc, nc.scalar, "act")
        self.dve = Eng(nc, nc.vector, "dve")
        self.pool = Eng(nc, nc.gpsimd, "pool")
        self.sp = Eng(nc, nc.sync, "sp")
        self.lastw = {}
        self.readers = {}
        self.floor = []
        self.dma_last = {}
        self.sems = {}
        self.dram = {}
        self.in_arrays = {}
        self.sb_cache = {}
        self.ps = nc.alloc_psum_tensor("psum", [128, 8, 512], F32)
        self.ps_i = 0
        self.reserved = set()
        self.uid = 0

    def sb(self, name, shape, dtype, off):
        key = (name, tuple(shape), str(dtype), off)
        if key not in self.sb_cache:
            self.uid += 1
            self.sb_cache[key] = self.nc.alloc_sbuf_tensor_at(f"{name}_{self.uid}", list(shape), dtype,
                                                              offset=SB_BASE + off)
        return self.sb_cache[key]

    def din(self, name, arr):
        if name not in self.dram:
            self.dram[name] = self.nc.dram_tensor(name, list(arr.shape), F32, kind="ExternalInput").ap()
            self.in_arrays[name] = arr
        return self.dram[name]

    def bank(self):
        for _ in range(16):
            b = self.ps_i % 8
            self.ps_i += 1
            if b not in self.reserved:
                return b
        raise RuntimeError("no free PSUM bank")

    def reserve(self, bk):
        self.reserved.add(bk)

    def unreserve(self, bk):
        self.reserved.discard(bk)

    def getsem(self, name):
        if name not in self.sems:
            self.sems[name] = Sem(self.nc, name)
        return self.sems[name]

    def _deps(self, reads, writes, use_floor=True):
        deps = list(self.floor) if use_floor else []
        for r in reads:
            deps.append(self.lastw.get(r))
        for w in writes:
            deps.append(self.lastw.get(w))
            deps.extend(self.readers.get(w, ()))
        return deps

    def _commit(self, tok, reads, writes):
        for w in writes:
            self.lastw[w] = tok
            self.readers[w] = []
        for r in reads:
            self.readers.setdefault(r, []).append(tok)

    def op(self, eng, fn, reads=(), writes=(), use_floor=True):
        eng.wait(self._deps(reads, writes, use_floor))
        ins = fn()
        tok = eng.mark(ins)
        self._commit(tok, reads, writes)
        return tok

    def dma(self, q, semname, out, in_, reads=(), writes=(), use_floor=True):
        q.wait(self._deps(reads, writes, use_floor))
        sem = self.getsem(semname)
        sem.cnt += 16
        q.e.dma_start(out=out, in_=in_).then_inc(sem.h, 16)
        tok = (sem, sem.cnt)
        if use_floor:
            self.dma_last[sem] = tok
        self._commit(tok, reads, writes)
        return tok

    def barrier(self):
        self.floor = [e.last for e in (self.pe, self.act, self.dve) if e.last is not None]
        self.floor += list(self.dma_last.values())


class WRing:
    def __init__(self, b, plan):
        self.b = b
        self.plan = plan
        self.issued = 0
        self.cur = 0
        self.views = []

    def _issue(self, j):
        b = self.b
        slot = j % NWSLOT
        views = []
        keys = []
        for i, (shape, src, eoff) in enumerate(self.plan[j]):
            v = b.sb("wr", shape, BF16, OFF_WRING + slot * WSLOT + eoff * 2)
            b.dma(b.pool, f"wr{slot}", v[:], src, writes=[("wr", slot, i)], use_floor=False)
            views.append(v)
            keys.append(("wr", slot, i))
        last = b.lastw[keys[-1]]
        for k in keys:
            b.lastw[k] = last
        return views, keys

    def get(self):
        j = self.cur
        self.cur += 1
        while self.issued < min(len(self.plan), j + NWSLOT - 1):
            self.views.append(self._issue(self.issued))
            self.issued += 1
        return self.views[j]

    def start(self):
        pass


class Prog:
    def __init__(self, inp_shapes_only, phases, arrays):
        self.b = B()
        self.A = arrays
        self.phases = phases
        b = self.b
        self.X = b.sb("X", [128, NCH, L], F32, OFF_X)
        o = OFF_CONST
        self.ident = b.sb("ident", [128, 128], BF16, o); o += 256
        self.ones1024 = b.sb("ones1024", [128, 128], BF16, o); o += 256
        self.ones512 = b.sb("ones512", [128, 128], BF16, o); o += 256
        self.onesf = b.sb("onesf", [128, 128], F32, o); o += 512
        self.cp = b.sb("cp", [128, CP_N], F32, o); o += CP_N * 4
        self.epsc = b.sb("epsc", [128, 8], F32, o); o += 32
        assert o <= OFF_CONST + CONST_BYTES, o

    def D(self, name):
        return self.b.din(name, self.A[name])

    def wplan(self, ph):
        kind = ph[0]
        tiles = []
        if kind in ("ffa", "ffb"):
            l = ph[1]
            wg = self.D(f"w_{kind}_gate")[l].rearrange("(k p) n -> p k n", p=128)
            wu = self.D(f"w_{kind}_up")[l].rearrange("(k p) n -> p k n", p=128)
            wd = self.D(f"w_{kind}_down")[l].rearrange("(f p) n -> p f n", p=128)
            for half in range(2):
                for f in range(NF):
                    tiles.append([([128, 8, 128], wg[:, :, f * 128:(f + 1) * 128], 0),
                                  ([128, 8, 128], wu[:, :, f * 128:(f + 1) * 128], 1024)])
                for d in range(NCH):
                    tiles.append([([128, 11, 128], wd[:, 0:11, d * 128:(d + 1) * 128], 0)])
                    tiles.append([([128, 11, 128], wd[:, 11:22, d * 128:(d + 1) * 128], 0)])
        elif kind == "mixhy":
            l = ph[1]
            wi = self.D("w_in")[l].rearrange("(k p) n -> p k n", p=128)
            for j in range(4):
                for s in range(3):
                    c0 = 1536 + s * 512 + j * 128
                    tiles.append([([128, 8, 128], wi[:, :, c0:c0 + 128], 0)])
        elif kind == "mixna":
            l = ph[1]
            wi = self.D("w_in")[l].rearrange("(k p) n -> p k n", p=128)
            for hp in range(4):
                for s in range(3):
                    c0 = s * 512 + hp * 128
                    tiles.append([([128, 8, 128], wi[:, :, c0:c0 + 128], 0)])
        elif kind == "wout":
            l = ph[1]
            wo = self.D("w_out")[l].rearrange("(k p) n -> p k n", p=128)
            for d in range(NCH):
                tiles.append([([128, 8, 128], wo[:, :, d * 128:(d + 1) * 128], 0)])
        elif kind == "ple":
            l = ph[1]
            wg = self.D("w_ple_gate")[l].rearrange("(k p) n -> p k n", p=128)
            wp = self.D("w_ple_proj")[l].rearrange("(k p) n -> p k n", p=128)
            for d in range(NCH):
                tiles.append([([128, 8, 128], wg[:, :, d * 128:(d + 1) * 128], 0),
                              ([128, 2, 128], wp[:, :, d * 128:(d + 1) * 128], 1024)])
        return tiles

    def build(self):
        b = self.b
        nc = b.nc
        plan = []
        for ph in self.phases:
            plan += self.wplan(ph)
        self.W = WRing(b, plan)
        self.W.start()
        b.op(b.dve, lambda: nc.vector.memset(self.ones1024[:], 1.0 / 1024.0), writes=["ones1024"])
        b.op(b.dve, lambda: nc.vector.memset(self.ones512[:], 1.0 / 512.0), writes=["ones512"])
        b.op(b.dve, lambda: nc.vector.memset(self.onesf[:], 1.0), writes=["onesf"])
        b.op(b.dve, lambda: nc.vector.memset(self.epsc[:], EPS), writes=["epsc"])
        b.op(b.dve, lambda: nc.vector.memset(self.epsc[:, 1:2], -math.pi), writes=["epsc"])
        self.KF = nc.dram_tensor("KFscr", [DEPTH, 16, 128, 2 * HY], F32, kind="Internal").ap()
        b.dma(b.sp, "cpl", self.cp[:], self.D("cpack"), writes=["cp"])
        identf = b.sb("identf", [128, 128], F32, OFF_R)
        b.dma(b.sp, "cpl2", identf[:], self.D("ident"), writes=["identf"])
        b.op(b.dve, lambda: nc.vector.tensor_copy(out=self.ident[:], in_=identf[:]), reads=["identf"], writes=["ident"])
        b.barrier()
        for ph in self.phases:
            kind = ph[0]
            if kind == "load":
                self.ph_load()
            elif kind == "filt":
                self.ph_filt(ph[1])
            elif kind in ("ffa", "ffb"):
                self.ph_ffn(kind, ph[1])
            elif kind == "mixhy":
                self.ph_mixhy(ph[1])
            elif kind == "mixna":
                self.ph_mixna(ph[1])
            elif kind == "wout":
                self.ph_wout(ph[1])
            elif kind == "ple":
                self.ph_ple(ph[1])
            elif kind == "store":
                self.ph_store(ph[1])
            elif kind == "zero":
                self.ph_zero(ph[1])
            else:
                raise ValueError(kind)
            b.barrier()
        assert self.W.cur == len(plan), (self.W.cur, len(plan))
        return nc

    def ph_load(self):
        b = self.b
        xT = self.D("xT").rearrange("(c p) t -> p c t", p=128)
        for c in range(NCH):
            b.dma(b.sp, f"xl{c}", self.X[:, c, :], xT[:, c, :], writes=[("X", c, n) for n in range(NT)])

    def ph_store(self, final):
        b = self.b
        nc = b.nc
        oT = nc.dram_tensor("outT", [D, L], F32, kind="ExternalOutput").ap().rearrange("(c p) t -> p c t", p=128)
        if final:
            H32 = b.sb("F32out", [128, NCH, TT], F32, OFF_R + 16384)
            self.norm(("g_final", None), out_fn=None, final_out=(H32, oT))
        else:
            toks = []
            for c in range(NCH):
                toks.append(b.dma(b.sp, f"xo{c}", oT[:, c, :], self.X[:, c, :], reads=[("X", c, n) for n in range(NT)]))
            b.sp.wait(toks)
            return
        b.sp.wait(list(b.dma_last.values()))

    def gcol(self, name, l, c):
        key = f"{name}{l}" if l is not None else name
        return self.cp[:, CP[key] + c: CP[key] + c + 1]

    def rstd(self, out, src, rkeys, wkeys):
        b = self.b
        nc = b.nc
        b.op(b.act, lambda: nc.scalar.activation(out=out, in_=src, func=AF.Sqrt, bias=self.epsc[:, 0:1], scale=1.0),
             reads=list(rkeys) + ["epsc"], writes=wkeys)
        b.op(b.dve, lambda: nc.vector.reciprocal(out=out, in_=out), reads=[], writes=wkeys)

    def norm(self, g, out_fn, final_out=None, nchunks=NCH, src=None):
        b = self.b
        nc = b.nc
        gname, l = g
        SQ = [b.sb("SQ", [128, NCH, TT], BF16, OFF_R + 32768 + i * 8192) for i in range(2)]
        RS = [b.sb("RS", [128, TT], F32, OFF_R + 32768 + 16384 + i * 2048) for i in range(2)]
        H = b.sb("H", [128, NCH, L], BF16, OFF_R)
        for n in range(NT):
            sq = SQ[n % 2]
            rs = RS[n % 2]
            ts = slice(n * TT, (n + 1) * TT)
            b.op(b.act, lambda: nc.scalar.activation(out=sq[:], in_=self.X[:, :, ts], func=AF.Square),
                 reads=[("X", c, n) for c in range(NCH)], writes=[("SQ", n % 2)])
            bk = b.bank()

            def mm():
                for c in range(NCH):
                    ins = nc.tensor.matmul(b.ps[:, bk, :], self.ones1024[:], sq[:, c, :], start=(c == 0), stop=(c == NCH - 1))
                return ins
            b.op(b.pe, mm, reads=[("SQ", n % 2), "ones1024"], writes=[("ps", bk)])
            self.rstd(rs[:], b.ps[:, bk, :], [("ps", bk)], [("RS", n % 2)])
            if final_out is None:
                for c in range(NCH):
                    b.op(b.dve, lambda c=c: nc.vector.scalar_tensor_tensor(
                        out=H[:, c, ts], in0=self.X[:, c, ts], scalar=self.gcol(gname, l, c), in1=rs[:],
                        op0=ALU.mult, op1=ALU.mult),
                        reads=[("X", c, n), ("RS", n % 2), "cp"], writes=[("H", c, n)])
            else:
                ob, oT = final_out
                for c in range(NCH):
                    b.op(b.dve, lambda c=c: nc.vector.scalar_tensor_tensor(
                        out=ob[:, c, :], in0=self.X[:, c, ts], scalar=self.gcol(gname, l, c), in1=rs[:],
                        op0=ALU.mult, op1=ALU.mult),
                        reads=[("X", c, n), ("RS", n % 2), "cp"], writes=[("OB", c)])
                    b.dma(b.sp, f"xo{c}", oT[:, c, ts], ob[:, c, :], reads=[("OB", c)])
        return H

    def ph_ffn(self, kind, l):
        b = self.b
        nc = b.nc
        H = self.norm((f"g_{kind}", l), None)
        A = b.sb("A", [128, NF, 1024], BF16, OFF_R + 32768 + 20480)
        S = [b.sb("S", [128, TT], F32, OFF_R + 32768 + 20480 + 45056 + i * 2048) for i in range(2)]
        assert 32768 + 20480 + 45056 + 4096 <= R_BYTES
        si = 0
        for half in range(2):
            for f in range(NF):
                (wg, wu), wkey = self.W.get()
                for t2 in range(2):
                    n = half * 2 + t2
                    ts = slice(n * TT, (n + 1) * TT)
                    bg = b.bank()
                    bu = b.bank()

                    def mm(w, bk):
                        for c in range(NCH):
                            ins = nc.tensor.matmul(b.ps[:, bk, :], w[:, c, :], H[:, c, ts], start=(c == 0), stop=(c == NCH - 1))
                        return ins
                    rd = [("H", c, n) for c in range(NCH)] + wkey
                    b.op(b.pe, lambda: mm(wg, bg), reads=rd, writes=[("ps", bg)])
                    b.op(b.pe, lambda: mm(wu, bu), reads=rd, writes=[("ps", bu)])
                    s = S[si % 2]
                    b.op(b.act, lambda: nc.scalar.activation(out=s[:], in_=b.ps[:, bg, :], func=AF.Silu),
                         reads=[("ps", bg)], writes=[("S", si % 2)])
                    b.op(b.dve, lambda: nc.vector.tensor_tensor(out=A[:, f, t2 * TT:(t2 + 1) * TT], in0=s[:], in1=b.ps[:, bu, :],
                                                                op=ALU.mult),
                         reads=[("S", si % 2), ("ps", bu)], writes=[("A", f, t2)])
                    si += 1
            for d in range(NCH):
                (w1,), k1 = self.W.get()
                (w2,), k2 = self.W.get()
                for t2 in range(2):
                    n = half * 2 + t2
                    ts = slice(n * TT, (n + 1) * TT)
                    bk = b.bank()

                    def mm():
                        for f in range(NF):
                            w = w1 if f < 11 else w2
                            ins = nc.tensor.matmul(b.ps[:, bk, :], w[:, f % 11, :], A[:, f, t2 * TT:(t2 + 1) * TT],
                                                   start=(f == 0), stop=(f == NF - 1))
                        return ins
                    b.op(b.pe, mm, reads=[("A", f, t2) for f in range(NF)] + k1 + k2, writes=[("ps", bk)])
                    b.op(b.dve, lambda: nc.vector.scalar_tensor_tensor(
                        out=self.X[:, d, ts], in0=b.ps[:, bk, :], scalar=0.5, in1=self.X[:, d, ts],
                        op0=ALU.mult, op1=ALU.add),
                        reads=[("ps", bk)], writes=[("X", d, n)])

    def ph_ple(self, l):
        b = self.b
        nc = b.nc
        H = self.norm(("g_ple", l), None)
        pT = b.sb("pT", [128, 2, L], BF16, OFF_R + 32768 + 20480)
        SG = [b.sb("SG", [128, TT], F32, OFF_R + 32768 + 20480 + 8192 + i * 2048) for i in range(2)]
        pd = self.D("pT")[l].rearrange("(k p) t -> p k t", p=128)
        b.dma(b.pool, "pTl", pT[:], pd, writes=["pT"])
        si = 0
        for d in range(NCH):
            (wg, wp), wkey = self.W.get()
            for n in range(NT):
                ts = slice(n * TT, (n + 1) * TT)
                bg = b.bank()
                bp = b.bank()

                def mmg():
                    for c in range(NCH):
                        ins = nc.tensor.matmul(b.ps[:, bg, :], wg[:, c, :], H[:, c, ts], start=(c == 0), stop=(c == NCH - 1))
                    return ins

                def mmp():
                    for c in range(2):
                        ins = nc.tensor.matmul(b.ps[:, bp, :], wp[:, c, :], pT[:, c, ts], start=(c == 0), stop=(c == 1))
                    return ins
                b.op(b.pe, mmg, reads=[("H", c, n) for c in range(NCH)] + wkey, writes=[("ps", bg)])
                b.op(b.pe, mmp, reads=["pT"] + wkey, writes=[("ps", bp)])
                sg = SG[si % 2]
                b.op(b.act, lambda: nc.scalar.activation(out=sg[:], in_=b.ps[:, bg, :], func=AF.Sigmoid),
                     reads=[("ps", bg)], writes=[("SG", si % 2)])
                b.op(b.dve, lambda: nc.vector.tensor_tensor(out=sg[:], in0=sg[:], in1=b.ps[:, bp, :], op=ALU.mult),
                     reads=[("ps", bp)], writes=[("SG", si % 2)])
                b.op(b.dve, lambda: nc.vector.tensor_tensor(out=self.X[:, d, ts], in0=self.X[:, d, ts], in1=sg[:], op=ALU.add),
                     reads=[("SG", si % 2)], writes=[("X", d, n)])
                si += 1

    def ph_filt(self, l):
        b = self.b
        nc = b.nc
        R0 = OFF_R
        zT = b.sb("zT", [128, L], F32, R0 + 0)
        hbuf = [b.sb("hbuf", [128, L], F32, R0 + 8192 + i * 8192) for i in range(2)]
        ARG = [b.sb("ARG", [128, TT], F32, R0 + 24576 + i * 2048) for i in range(2)]
        wf = [b.sb("wf1", [128, 64], F32, R0 + 28672), b.sb("wf2", [128, 64], F32, R0 + 28928),
              b.sb("wf3", [128, 64], F32, R0 + 29184)]
        wf4 = b.sb("wf4", [128, 1024], F32, R0 + 29440)
        fbt = b.sb("fbt", [128, 4], F32, R0 + 33536)
        MSK = [b.sb("MSK", [128, TT], F32, R0 + 34816)] * 2
        hs = b.sb("hs", [128, 16, HY], BF16, R0 + 36864)
        hd = b.sb("hd", [128, 16, HY], BF16, R0 + 53248)
        DEC = [b.sb("DEC", [128, HY], F32, R0 + 69632 + i * 2048) for i in range(2)]
        HF = [b.sb("HF", [128, 2, HY], F32, R0 + 73728)] * 2
        KO = [b.sb("KO", [128, 2, HY], F32, R0 + 77824 + i * 4096) for i in range(2)]
        inv = b.sb("inv", [128, HY], F32, R0 + 86016)
        biasb = b.sb("biasb", [128, HY], F32, R0 + 88064)
        brow = b.sb("brow", [128, HY], F32, R0 + 90112)
        FWT = [b.sb("FWT", [128, 16, 2, 128], BF16, R0 + 92160 + i * 8192) for i in range(2)]
        assert 92160 + 16384 <= R_BYTES
        KF = self.KF
        b.dma(b.sp, "fl0", zT[0:FILT_EMB, :], self.D("zT"), writes=["zT"])
        b.dma(b.sp, "fl1", wf[0][0:FILT_EMB, :], self.D("w_f1")[l], writes=["wf0"])
        b.dma(b.sp, "fl2", wf[1][0:64, :], self.D("w_f2")[l], writes=["wf1"])
        b.dma(b.sp, "fl3", wf[2][0:64, :], self.D("w_f3")[l], writes=["wf2"])
        b.dma(b.sp, "fl4", wf4[0:64, :], self.D("w_f4")[l], writes=["wf4"])
        b.dma(b.sp, "fl5", brow[0:1, :], self.D("hy_bias")[l:l + 1, :], writes=["brow"])
        fq = self.cp[0:64, CP[f"freq{l}"]:CP[f"freq{l}"] + 1]
        for i in range(3):
            bi = self.cp[0:64, CP[f"b_f{i + 1}{l}"]:CP[f"b_f{i + 1}{l}"] + 1]
            b.op(b.dve, lambda: nc.vector.tensor_tensor(out=fbt[0:64, i:i + 1], in0=bi, in1=fq, op=ALU.mult),
                 reads=["cp"], writes=[("fbt", i)])
        if _STOP <= 1:
            return
        hin = zT
        kin = FILT_EMB
        ai = 0
        for i in range(3):
            hout = hbuf[i % 2]
            for n in range(NT):
                ts = slice(n * TT, (n + 1) * TT)
                bk = b.bank()
                b.op(b.pe, lambda: nc.tensor.matmul(b.ps[0:64, bk, :], wf[i][0:kin, :], hin[0:kin, ts], start=True, stop=True),
                     reads=[f"wf{i}", ("hin", i, n)] + (["zT"] if i == 0 else []), writes=[("ps", bk)])
                arg = ARG[ai % 2]
                b.op(b.dve, lambda: nc.vector.tensor_scalar(out=arg[0:64, :], in0=b.ps[0:64, bk, :], scalar1=fq,
                                                            scalar2=fbt[0:64, i:i + 1], op0=ALU.mult, op1=ALU.add),
                     reads=[("ps", bk), ("fbt", i), "cp"], writes=[("ARG", ai % 2)])
                msk = MSK[ai % 2]
                b.op(b.dve, lambda: nc.vector.tensor_single_scalar(out=msk[0:64, :], in_=arg[0:64, :], scalar=math.pi, op=ALU.is_gt),
                     reads=[("ARG", ai % 2)], writes=["MSK"])
                b.op(b.dve, lambda: nc.vector.scalar_tensor_tensor(out=arg[0:64, :], in0=msk[0:64, :], scalar=-2.0 * math.pi,
                                                                   in1=arg[0:64, :], op0=ALU.mult, op1=ALU.add),
                     reads=["MSK"], writes=[("ARG", ai % 2)])
                b.op(b.dve, lambda: nc.vector.tensor_single_scalar(out=msk[0:64, :], in_=arg[0:64, :], scalar=-math.pi, op=ALU.is_lt),
                     reads=[("ARG", ai % 2)], writes=["MSK"])
                b.op(b.dve, lambda: nc.vector.scalar_tensor_tensor(out=arg[0:64, :], in0=msk[0:64, :], scalar=2.0 * math.pi,
                                                                   in1=arg[0:64, :], op0=ALU.mult, op1=ALU.add),
                     reads=["MSK"], writes=[("ARG", ai % 2)])
                b.op(b.act, lambda: nc.scalar.activation(out=hout[0:64, ts], in_=arg[0:64, :], func=AF.Sin),
                     reads=[("ARG", ai % 2)], writes=[("hin", i + 1, n)])
                ai += 1
            hin = hout
            kin = 64
        h3 = hin
        if _STOP <= 2:
            return
        babs = b.bank()
        b.reserve(babs)
        dec = self.D("decay")
        for j in range(16):
            bf_ = b.bank()
            bb_ = b.bank()
            b.op(b.pe, lambda: nc.tensor.matmul(b.ps[:, bf_, :], h3[0:64, j * 128:(j + 1) * 128], wf4[0:64, 0:HY], start=True, stop=True),
                 reads=[("hin", 3, j // 4), "wf4"], writes=[("ps", bf_)])
            b.op(b.pe, lambda: nc.tensor.matmul(b.ps[:, bb_, :], h3[0:64, j * 128:(j + 1) * 128], wf4[0:64, HY:2 * HY], start=True, stop=True),
                 reads=[("hin", 3, j // 4), "wf4"], writes=[("ps", bb_)])
            dc = DEC[j % 2]
            b.dma(b.sp, f"dec{j % 2}", dc[:], dec[:, j, :], writes=[("DEC", j % 2)])
            hf = HF[j % 2]
            b.op(b.dve, lambda: nc.vector.tensor_tensor(out=hf[:, 0, :], in0=b.ps[:, bf_, :], in1=dc[:], op=ALU.mult),
                 reads=[("ps", bf_), ("DEC", j % 2)], writes=["HF"])
            b.op(b.dve, lambda: nc.vector.tensor_tensor(out=hf[:, 1, :], in0=b.ps[:, bb_, :], in1=dc[:], op=ALU.mult),
                 reads=[("ps", bb_), ("DEC", j % 2)], writes=["HF"])
            if j == 0:
                b.op(b.dve, lambda: nc.vector.tensor_tensor(out=hf[0:1, 0, :], in0=hf[0:1, 0, :], in1=hf[0:1, 1, :], op=ALU.add),
                     reads=[], writes=["HF"])
                b.op(b.dve, lambda: nc.vector.memset(hf[0:1, 1, :], 0.0), reads=[], writes=["HF"])
            b.op(b.dve, lambda: nc.vector.tensor_tensor(out=hs[:, j, :], in0=hf[:, 0, :], in1=hf[:, 1, :], op=ALU.add),
                 reads=["HF"], writes=[("hs", j)])
            b.op(b.dve, lambda: nc.vector.tensor_tensor(out=hd[:, j, :], in0=hf[:, 0, :], in1=hf[:, 1, :], op=ALU.subtract),
                 reads=["HF"], writes=[("hd", j)])
            b.op(b.act, lambda: nc.scalar.activation(out=hf[:], in_=hf[:], func=AF.Abs),
                 reads=[], writes=["HF"])

            def mmabs():
                nc.tensor.matmul(b.ps[:, babs, :], self.onesf[:], hf[:, 0, :], start=(j == 0), stop=False)
                return nc.tensor.matmul(b.ps[:, babs, :], self.onesf[:], hf[:, 1, :], start=False, stop=(j == 15))
            b.op(b.pe, mmabs, reads=["HF", "onesf"], writes=[("ps", babs)])
        if _STOP <= 3:
            b.unreserve(babs)
            return
        b.op(b.dve, lambda: nc.vector.reciprocal(out=inv[:], in_=b.ps[:, babs, :]), reads=[("ps", babs)], writes=["inv"])
        b.unreserve(babs)
        bk = b.bank()
        b.op(b.pe, lambda: nc.tensor.matmul(b.ps[:, bk, :], self.onesf[0:1, :], brow[0:1, :], start=True, stop=True),
             reads=["brow", "onesf"], writes=[("ps", bk)])
        b.op(b.dve, lambda: nc.vector.tensor_copy(out=biasb[:], in_=b.ps[:, bk, :]), reads=[("ps", bk)], writes=["biasb"])
        if _STOP <= 4:
            return
        fw = self.D("FW")
        for m in range(16):
            fwt = FWT[m % 2]
            b.dma(b.pool, f"fwt{m % 2}", fwt[:].rearrange("p a c f -> p (a c f)"), fw[m], writes=[("FWT", m % 2)])
            bK = b.bank()
            bQ = b.bank()

            def mmk(cs, bk_, rhs):
                for kc in range(16):
                    ins = nc.tensor.matmul(b.ps[:, bk_, :], fwt[:, kc, cs, :], rhs[:, kc, :], start=(kc == 0), stop=(kc == 15))
                return ins
            b.op(b.pe, lambda: mmk(0, bK, hs), reads=[("FWT", m % 2)] + [("hs", j) for j in range(16)], writes=[("ps", bK)])
            b.op(b.pe, lambda: mmk(1, bQ, hd), reads=[("FWT", m % 2)] + [("hd", j) for j in range(16)], writes=[("ps", bQ)])
            ko = KO[m % 2]
            b.op(b.dve, lambda: nc.vector.tensor_tensor(out=ko[:, 0, :], in0=b.ps[:, bK, :], in1=inv[:], op=ALU.mult),
                 reads=[("ps", bK), "inv"], writes=[("KO", m % 2)])
            b.op(b.dve, lambda: nc.vector.tensor_tensor(out=ko[:, 0, :], in0=ko[:, 0, :], in1=biasb[:], op=ALU.add),
                 reads=["biasb"], writes=[("KO", m % 2)])
            b.op(b.dve, lambda: nc.vector.tensor_tensor(out=ko[:, 1, :], in0=b.ps[:, bQ, :], in1=inv[:], op=ALU.mult),
                 reads=[("ps", bQ), "inv"], writes=[("KO", m % 2)])
            if m == 0 and 'nonyq' not in _VAR:
                bN = b.bank()

                def mmn():
                    for kc in range(16):
                        ins = nc.tensor.matmul(b.ps[0:1, bN, :], fwt[:, kc, 1, 0:1], hs[:, kc, :], start=(kc == 0), stop=(kc == 15))
                    return ins
                b.op(b.pe, mmn, reads=[("FWT", m % 2)] + [("hs", j) for j in range(16)], writes=[("ps", bN)])
                b.op(b.dve, lambda: nc.vector.tensor_tensor(out=ko[0:1, 1, :], in0=b.ps[0:1, bN, :], in1=inv[0:1, :], op=ALU.mult),
                     reads=[("ps", bN), "inv"], writes=[("KO", m % 2)])
                b.op(b.dve, lambda: nc.vector.tensor_tensor(out=ko[0:1, 1, :], in0=ko[0:1, 1, :], in1=biasb[0:1, :], op=ALU.add),
                     reads=["biasb"], writes=[("KO", m % 2)])
            if 'nostore' not in _VAR:
                b.dma(b.sp, f"kfo{m % 2}", KF[l, m], ko[:].rearrange("p a c -> p (a c)"), reads=[("KO", m % 2)], writes=[("KF", l, m)])

    def ph_mixhy(self, l):
        b = self.b
        nc = b.nc
        R0 = OFF_R
        H = self.norm(("g_mix", l), None)
        b.barrier()
        U = [b.sb("U", [128, L + 2], F32, R0 + 32768 + i * 8256) for i in range(3)]
        T = [b.sb("T", [128, L], F32, R0 + 57536 + i * 8192) for i in range(2)]
        vxf = b.sb("vxf", [128, L], BF16, R0 + 32768 + 64)
        sx0 = b.sb("sx0", [128, 4, L], BF16, R0 + 73920)
        vxT = b.sb("vxT", [128, 16, HY], BF16, R0 + OFF_YNHY)
        assert OFF_YNHY + 16384 <= R_BYTES
        for i in range(3):
            b.op(b.dve, lambda: nc.vector.memset(U[i][:, 0:1], 0.0), writes=[("Uh", i)])
            b.op(b.dve, lambda: nc.vector.memset(U[i][:, L + 1:L + 2], 0.0), writes=[("Uh", i)])
        wsc = CP[f"wsc{l}"]
        bsc = CP[f"bsc{l}"]
        for j in range(4):
            for s_ in range(3):
                (w,), wkey = self.W.get()
                for n in range(NT):
                    ts = slice(n * TT, (n + 1) * TT)
                    bk = b.bank()

                    def mm():
                        for c in range(NCH):
                            ins = nc.tensor.matmul(b.ps[:, bk, :], w[:, c, :], H[:, c, ts], start=(c == 0), stop=(c == NCH - 1))
                        return ins
                    b.op(b.pe, mm, reads=[("H", c, n) for c in range(NCH)] + wkey, writes=[("ps", bk)])
                    b.op(b.act, lambda: nc.scalar.copy(out=U[s_][:, 1 + n * TT:1 + (n + 1) * TT], in_=b.ps[:, bk, :]),
                         reads=[("ps", bk)], writes=[("U", s_, n)])
                cc = s_ * 4 + j
                w0 = self.cp[:, wsc + cc:wsc + cc + 1]
                w1 = self.cp[:, wsc + 12 + cc:wsc + 12 + cc + 1]
                w2 = self.cp[:, wsc + 24 + cc:wsc + 24 + cc + 1]
                bb = self.cp[:, bsc + cc:bsc + cc + 1]
                tt_ = T[0] if s_ < 2 else T[1]
                ukeys = [("U", s_, n) for n in range(NT)] + [("Uh", s_)]
                b.op(b.act, lambda: nc.scalar.activation(out=tt_[:], in_=U[s_][:, 1:L + 1], func=AF.Identity, bias=bb, scale=w1),
                     reads=ukeys + ["cp"], writes=[("T", 0 if s_ < 2 else 1)])
                b.op(b.dve, lambda: nc.vector.scalar_tensor_tensor(out=tt_[:], in0=U[s_][:, 0:L], scalar=w0, in1=tt_[:],
                                                                   op0=ALU.mult, op1=ALU.add),
                     reads=ukeys + ["cp"], writes=[("T", 0 if s_ < 2 else 1)])
                dst = sx0[:, j, :] if s_ == 0 else tt_[:]
                dkeys = [("sx0", j)] if s_ == 0 else []
                b.op(b.dve, lambda: nc.vector.scalar_tensor_tensor(out=dst, in0=U[s_][:, 2:L + 2], scalar=w2, in1=tt_[:],
                                                                   op0=ALU.mult, op1=ALU.add),
                     reads=ukeys + ["cp"], writes=[("T", 0 if s_ < 2 else 1)] + dkeys)
            b.op(b.dve, lambda: nc.vector.tensor_tensor(out=vxf[:], in0=T[0][:], in1=T[1][:], op=ALU.mult),
                 reads=[("T", 0), ("T", 1)], writes=["vxf"] + [("U", 0, n) for n in range(NT)] + [("Uh", 0)])
            for tq in range(4):
                bk = b.bank()

                def mmt():
                    for i in range(4):
                        tc_ = tq * 4 + i
                        ins = nc.tensor.matmul(b.ps[:, bk, i * 128:(i + 1) * 128], vxf[:, tc_ * 128:(tc_ + 1) * 128], self.ident[:],
                                               start=True, stop=True)
                    return ins
                b.op(b.pe, mmt, reads=["vxf", "ident"], writes=[("ps", bk)])
                b.op(b.act, lambda: nc.scalar.copy(out=vxT[:, tq * 4:(tq + 1) * 4, j * 128:(j + 1) * 128],
                                                   in_=b.ps[:, bk, :].rearrange("p (a c) -> p a c", a=4)),
                     reads=[("ps", bk)], writes=[("vxT", tq, j)])
        b.barrier()
        YF = b.sb("YF", [128, 16, 2, HY], BF16, R0 + 0)
        FWT = [b.sb("FWT", [128, 16, 2, 128], BF16, R0 + 32768 + i * 8192) for i in range(2)]
        KFT = [b.sb("KFT", [128, 2, HY], F32, R0 + 49152 + i * 4096) for i in range(2)]
        TMP = [b.sb("TMP", [128, 2, HY], F32, R0 + 57344 + i * 4096) for i in range(2)]
        fw = self.D("FW")
        KF = self.KF
        vkeys = [("vxT", tq, j) for tq in range(4) for j in range(4)]
        for m in range(16):
            fwt = FWT[m % 2]
            b.dma(b.pool, f"fwt{m % 2}", fwt[:].rearrange("p a c f -> p (a c f)"), fw[m], writes=[("FWT", m % 2)])
            kft = KFT[m % 2]
            b.dma(b.sp, f"kfi{m % 2}", kft[:].rearrange("p a c -> p (a c)"), KF[l, m], reads=[("KF", l, m)], writes=[("KFT", m % 2)])
            bV = b.bank()
            bW = b.bank()

            def mmk(cs, bk_):
                for kc in range(16):
                    ins = nc.tensor.matmul(b.ps[:, bk_, :], fwt[:, kc, cs, :], vxT[:, kc, :], start=(kc == 0), stop=(kc == 15))
                return ins
            b.op(b.pe, lambda: mmk(0, bV), reads=[("FWT", m % 2)] + vkeys, writes=[("ps", bV)])
            b.op(b.pe, lambda: mmk(1, bW), reads=[("FWT", m % 2)] + vkeys, writes=[("ps", bW)])
            tmp = TMP[m % 2]
            kk = [("KFT", m % 2)]
            b.op(b.dve, lambda: nc.vector.tensor_tensor(out=tmp[:, 0, :], in0=b.ps[:, bV, :], in1=kft[:, 0, :], op=ALU.mult),
                 reads=[("ps", bV)] + kk, writes=[("TMP", m % 2, 0)])
            b.op(b.dve, lambda: nc.vector.tensor_tensor(out=tmp[:, 1, :], in0=b.ps[:, bW, :], in1=kft[:, 1, :], op=ALU.mult),
                 reads=[("ps", bW)] + kk, writes=[("TMP", m % 2, 1)])
            b.op(b.dve, lambda: nc.vector.tensor_tensor(out=YF[:, m, 0, :], in0=tmp[:, 0, :], in1=tmp[:, 1, :], op=ALU.subtract),
                 reads=[("TMP", m % 2, 0), ("TMP", m % 2, 1)], writes=[("YF", m, 0)])
            b.op(b.dve, lambda: nc.vector.tensor_tensor(out=tmp[:, 0, :], in0=b.ps[:, bV, :], in1=kft[:, 1, :], op=ALU.mult),
                 reads=[("ps", bV)] + kk, writes=[("TMP", m % 2, 0)])
            b.op(b.dve, lambda: nc.vector.tensor_tensor(out=tmp[:, 1, :], in0=b.ps[:, bW, :], in1=kft[:, 0, :], op=ALU.mult),
                 reads=[("ps", bW)] + kk, writes=[("TMP", m % 2, 1)])
            b.op(b.dve, lambda: nc.vector.tensor_tensor(out=YF[:, m, 1, :], in0=tmp[:, 0, :], in1=tmp[:, 1, :], op=ALU.add),
                 reads=[("TMP", m % 2, 0), ("TMP", m % 2, 1)], writes=[("YF", m, 1)])
            if m == 0:
                b.op(b.dve, lambda: nc.vector.tensor_tensor(out=YF[0:1, 0, 0, :], in0=b.ps[0:1, bV, :], in1=kft[0:1, 0, :], op=ALU.mult),
                     reads=[("ps", bV)] + kk, writes=[("YF", m, 0)])
                b.op(b.dve, lambda: nc.vector.tensor_tensor(out=YF[0:1, 0, 1, :], in0=b.ps[0:1, bW, :], in1=kft[0:1, 1, :], op=ALU.mult),
                     reads=[("ps", bW)] + kk, writes=[("YF", m, 1)])
        b.barrier()
        GR = [b.sb("GR", [128, 2, TT], BF16, R0 + 32768 + i * 2048) for i in range(4)]
        zb = b.sb("zb", [128, 4, TT], F32, R0 + 40960)
        sqz = b.sb("sqz", [128, 4, TT], BF16, R0 + 49152)
        rs2 = b.sb("rs2", [128, TT], F32, R0 + 53248)
        YNH = b.sb("YNH", [128, 4, L], BF16, R0 + OFF_YNHY)
        gi = self.D("GI")
        yfk = [("YF", m, ri) for m in range(16) for ri in range(2)]
        gidx = 0
        for n in range(NT):
            ts = slice(n * TT, (n + 1) * TT)
            bks = [b.bank() for _ in range(4)]
            for bk in bks:
                b.reserve(bk)
            for m in range(16):
                g = GR[gidx % 4]
                b.dma(b.pool, f"gr{gidx % 4}", g[:].rearrange("p a c -> p (a c)"), gi[n, m], writes=[("GR", gidx % 4)])

                def mmi():
                    for ri in range(2):
                        for j in range(4):
                            ins = nc.tensor.matmul(b.ps[:, bks[j], :], YF[:, m, ri, j * 128:(j + 1) * 128], g[:, ri, :],
                                                   start=(m == 0 and ri == 0), stop=(m == 15 and ri == 1))
                    return ins
                b.op(b.pe, mmi, reads=[("GR", gidx % 4)] + (yfk if m == 0 else []), writes=[("ps", bk) for bk in bks])
                gidx += 1
            for j in range(4):
                b.op(b.dve, lambda: nc.vector.tensor_tensor(out=zb[:, j, :], in0=b.ps[:, bks[j], :], in1=sx0[:, j, ts], op=ALU.mult),
                     reads=[("ps", bks[j]), ("sx0", j)], writes=[("zb", j)])
                b.unreserve(bks[j])
            b.op(b.act, lambda: nc.scalar.activation(out=sqz[:], in_=zb[:], func=AF.Square),
                 reads=[("zb", j) for j in range(4)], writes=["sqz"])
            bk = b.bank()

            def mms():
                for j in range(4):
                    ins = nc.tensor.matmul(b.ps[:, bk, :], self.ones512[:], sqz[:, j, :], start=(j == 0), stop=(j == 3))
                return ins
            b.op(b.pe, mms, reads=["sqz", "ones512"], writes=[("ps", bk)])
            self.rstd(rs2[:], b.ps[:, bk, :], [("ps", bk)], ["rs2"])
            for j in range(4):
                b.op(b.dve, lambda: nc.vector.scalar_tensor_tensor(out=YNH[:, j, ts], in0=zb[:, j, :], scalar=self.gcol("g_out", l, 4 + j),
                                                                   in1=rs2[:], op0=ALU.mult, op1=ALU.mult),
                     reads=[("zb", j), "rs2", "cp"], writes=[("YNH", j, n)])

    def ph_mixna(self, l):
        b = self.b
        nc = b.nc
        R0 = OFF_R
        H = self.norm(("g_mix", l), None)
        b.barrier()
        qT = b.sb("qT", [128, L], BF16, R0 + 32768)
        kT = b.sb("kT", [128, L], BF16, R0 + 36864)
        vE = b.sb("vE", [128, 16, 2, 65], BF16, R0 + 40960)
        vO = b.sb("vO", [128, 15, 2, 65], BF16, R0 + 45120)
        yna = b.sb("yna", [128, 16, 512], BF16, R0 + 49152)
        E = b.sb("E", [128, NHEAD, NDR, 64], BF16, R0 + 65536)
        BST = [b.sb("BST", [128, NDR * 64], F32, R0 + 79872)] * 2
        PEX = [b.sb("PEX", [128, 512], BF16, R0 + 79872 + i * 1024) for i in range(4)]
        PP = [b.sb("PP", [128, 512], BF16, R0 + 83968 + i * 1024) for i in range(4)]
        REC = [b.sb("REC", [128, 2], F32, R0 + 88064 + i * 32) for i in range(2)]
        ss = b.sb("ss", [128, 16], F32, R0 + 88128)
        rs16 = b.sb("rs16", [128, 16], F32, R0 + 88192)
        YT = [b.sb("YT", [128, 512], BF16, R0 + 83968 + i * 1024) for i in range(2)]
        junk = b.sb("junk", [128, 512], F32, R0 + 79872)
        assert 88256 <= OFF_YNHY
        tab = self.D("btab")
        for h in range(NHEAD):
            bst = BST[h % 2]
            b.dma(b.sp, "bst", bst[:], tab[l, h].rearrange("p d q -> p (d q)"), writes=["BST"])
            b.op(b.act, lambda: nc.scalar.activation(out=E[:, h, :, :].rearrange("p d q -> p (d q)"), in_=bst[:], func=AF.Exp),
                 reads=["BST"], writes=[("E", h)])
        b.barrier()
        b.op(b.dve, lambda: nc.vector.memset(vE[:, :, :, 64:65], 1.0), writes=["vE1"])
        b.op(b.dve, lambda: nc.vector.memset(vO[:, :, :, 64:65], 1.0), writes=["vO1"])
        pi = 0
        for hp in range(4):
            (wq,), kq = self.W.get()
            (wk,), kk_ = self.W.get()
            for (w, wkey, dst, dname) in ((wq, kq, qT, "qT"), (wk, kk_, kT, "kT")):
                for n in range(NT):
                    ts = slice(n * TT, (n + 1) * TT)
                    bk = b.bank()

                    def mm():
                        for c in range(NCH):
                            ins = nc.tensor.matmul(b.ps[:, bk, :], w[:, c, :], H[:, c, ts], start=(c == 0), stop=(c == NCH - 1))
                        return ins
                    b.op(b.pe, mm, reads=[("H", c, n) for c in range(NCH)] + wkey, writes=[("ps", bk)])
                    if dname == "qT":
                        b.op(b.act, lambda: nc.scalar.copy(out=dst[:, ts], in_=b.ps[:, bk, :]), reads=[("ps", bk)], writes=[(dname, n)])
                    else:
                        b.op(b.dve, lambda: nc.vector.tensor_copy(out=dst[:, ts], in_=b.ps[:, bk, :]), reads=[("ps", bk)], writes=[(dname, n)])
            (wv,), kv = self.W.get()
            for (vt, ntile, toff, vname) in ((vE, 16, 0, "vE"), (vO, 15, 64, "vO")):
                for g0 in range(0, ntile, 4):
                    cnt = min(4, ntile - g0)
                    bk = b.bank()

                    def mmv():
                        for i in range(cnt):
                            t0 = toff + (g0 + i) * 128
                            for c in range(NCH):
                                ins = nc.tensor.matmul(b.ps[:, bk, i * 128:(i + 1) * 128], H[:, c, t0:t0 + 128], wv[:, c, :],
                                                       start=(c == 0), stop=(c == NCH - 1))
                        return ins
                    nn = sorted(set([(toff + (g0 + i) * 128) // TT for i in range(cnt)] + [(toff + (g0 + i) * 128 + 127) // TT for i in range(cnt)]))
                    b.op(b.pe, mmv, reads=[("H", c, n) for c in range(NCH) for n in nn] + kv, writes=[("ps", bk)])
                    b.op(b.act, lambda: nc.scalar.copy(out=vt[:, g0:g0 + cnt, :, 0:64],
                                                       in_=b.ps[:, bk, 0:cnt * 128].rearrange("p (a h d) -> p a h d", a=cnt, h=2)),
                         reads=[("ps", bk)], writes=[(vname, g0 // 4), vname + "1"])
            vkeys = {"vE": [("vE", g) for g in range(4)] + ["vE1"], "vO": [("vO", g) for g in range(4)] + ["vO1"]}
            for jp in range(NROWS // 2 if _NAS > 1 else 0):
                rows = (2 * jp, 2 * jp + 1)
                info = []
                for r in rows:
                    rs_ = min(max(r - 4, 0), NROWS - 8)
                    if rs_ % 2 == 0:
                        info.append((r, rs_ * 64, (rs_ - r) + 7, vE, "vE", rs_ // 2))
                    else:
                        info.append((r, rs_ * 64, (rs_ - r) + 7, vO, "vO", (rs_ - 1) // 2))
                bs = [b.bank(), b.bank()]
                for hh in range(2):
                    def mms():
                        for ri, (r, ktok, d0, vt, vn, jv0) in enumerate(info):
                            for kc in range(4):
                                ins = nc.tensor.matmul(b.ps[:, bs[hh], ri * 256 + kc * 64: ri * 256 + (kc + 1) * 64],
                                                       kT[hh * 64:(hh + 1) * 64, ktok + kc * 128: ktok + (kc + 1) * 128],
                                                       qT[hh * 64:(hh + 1) * 64, r * 64:(r + 1) * 64], start=True, stop=True)
                        return ins
                    b.op(b.pe, mms, reads=[("qT", n) for n in range(NT)] + [("kT", n) for n in range(NT)], writes=[("ps", bs[hh])])
                pbuf = []
                for hh in range(2):
                    pex = PEX[pi % 4]
                    pp = PP[pi % 4]
                    b.op(b.act, lambda: nc.scalar.activation(out=pex[:], in_=b.ps[:, bs[hh], :], func=AF.Exp, scale=0.125),
                         reads=[("ps", bs[hh])], writes=[("PEX", pi % 4)])
                    if _NAS > 2:
                        for ri, (r, ktok, d0, vt, vn, jv0) in enumerate(info):
                            b.op(b.dve, lambda: nc.vector.tensor_tensor(
                                out=pp[:, ri * 256:(ri + 1) * 256].rearrange("p (k q) -> p k q", k=4),
                                in0=pex[:, ri * 256:(ri + 1) * 256].rearrange("p (k q) -> p k q", k=4),
                                in1=E[:, 2 * hp + hh, d0:d0 + 7:2, :], op=ALU.mult),
                                reads=[("PEX", pi % 4), ("E", 2 * hp + hh)], writes=[("PP", pi % 4, ri)])
                    pbuf.append((pp, pi % 4))
                    pi += 1
                if _NAS <= 3:
                    continue
                bo = [b.bank(), b.bank()]
                for ri, (r, ktok, d0, vt, vn, jv0) in enumerate(info):
                    po = ri * 64

                    def mmo():
                        for hh in range(2):
                            pp, pidx = pbuf[hh]
                            for kc in range(4):
                                ins = nc.tensor.matmul(b.ps[po:po + 64, bo[ri], hh * 65:(hh + 1) * 65],
                                                       pp[:, ri * 256 + kc * 64: ri * 256 + (kc + 1) * 64],
                                                       vt[:, jv0 + kc, hh, :], start=(kc == 0), stop=(kc == 3))
                        return ins
                    b.op(b.pe, mmo, reads=[("PP", pbuf[0][1], ri), ("PP", pbuf[1][1], ri)] + vkeys[vn], writes=[("ps", bo[ri])])
                if _NAS <= 4:
                    continue
                rec = REC[jp % 2]
                for ri in range(2):
                    po = ri * 64
                    b.op(b.dve, lambda: nc.vector.reciprocal(out=rec[po:po + 64, :], in_=b.ps[po:po + 64, bo[ri], 64:130:65]),
                         reads=[("ps", bo[ri])], writes=[("REC", jp % 2, ri)])
                    for hh in range(2):
                        b.op(b.dve, lambda: nc.vector.tensor_scalar(out=yna[po:po + 64, jp, hp * 128 + hh * 64: hp * 128 + (hh + 1) * 64],
                                                                    in0=b.ps[po:po + 64, bo[ri], hh * 65: hh * 65 + 64],
                                                                    scalar1=rec[po:po + 64, hh:hh + 1], scalar2=None, op0=ALU.mult),
                             reads=[("ps", bo[ri]), ("REC", jp % 2, ri)], writes=[("yna", jp, ri)])
        b.barrier()
        if _NAS <= 5:
            return
        YNN = b.sb("YNN", [128, 4, L], BF16, R0 + 0)
        for j in range(16):
            b.op(b.act, lambda: nc.scalar.activation(out=junk[:], in_=yna[:, j, :], func=AF.Square),
                 reads=[("yna", j, 0), ("yna", j, 1)], writes=["junk"])
            b.op(b.dve, lambda: nc.vector.reduce_sum(out=ss[:, j:j + 1], in_=junk[:], axis=mybir.AxisListType.X),
                 reads=["junk"], writes=[("ss", j)])
        b.op(b.act, lambda: nc.scalar.activation(out=rs16[:], in_=ss[:], func=AF.Sqrt, bias=self.epsc[:, 0:1], scale=1.0 / 512.0),
             reads=[("ss", j) for j in range(16)] + ["epsc"], writes=["rs16"])
        b.op(b.dve, lambda: nc.vector.reciprocal(out=rs16[:], in_=rs16[:]), reads=[], writes=["rs16"])
        for j in range(16):
            yt = YT[j % 2]
            b.op(b.dve, lambda: nc.vector.tensor_scalar(out=yt[:], in0=yna[:, j, :], scalar1=rs16[:, j:j + 1], scalar2=None, op0=ALU.mult),
                 reads=[("yna", j, 0), ("yna", j, 1), "rs16"], writes=[("YT", j % 2)])
            bk = b.bank()

            def mmt():
                for c in range(4):
                    ins = nc.tensor.matmul(b.ps[:, bk, c * 128:(c + 1) * 128], yt[:, c * 128:(c + 1) * 128], self.ident[:], start=True, stop=True)
                return ins
            b.op(b.pe, mmt, reads=[("YT", j % 2), "ident"], writes=[("ps", bk)])
            for c in range(4):
                b.op(b.dve, lambda c=c: nc.vector.tensor_scalar(out=YNN[:, c, j * 128:(j + 1) * 128], in0=b.ps[:, bk, c * 128:(c + 1) * 128],
                                                                scalar1=self.gcol("g_out", l, c), scalar2=None, op0=ALU.mult),
                     reads=[("ps", bk), "cp"], writes=[("YNN", c, j // 4)])

    def ph_zero(self, which):
        b = self.b
        nc = b.nc
        if which == "na":
            t = b.sb("YNN", [128, 4, L], BF16, OFF_R + 0)
            b.op(b.dve, lambda: nc.vector.memset(t[:], 0.0), writes=[("YNN", c, n) for c in range(4) for n in range(NT)])
        else:
            t = b.sb("YNH", [128, 4, L], BF16, OFF_R + OFF_YNHY)
            b.op(b.dve, lambda: nc.vector.memset(t[:], 0.0), writes=[("YNH", c, n) for c in range(4) for n in range(NT)])

    def ph_wout(self, l):
        b = self.b
        nc = b.nc
        YNN = b.sb("YNN", [128, 4, L], BF16, OFF_R + 0)
        YNH = b.sb("YNH", [128, 4, L], BF16, OFF_R + OFF_YNHY)
        for d in range(NCH):
            (w,), wkey = self.W.get()
            for n in range(NT):
                ts = slice(n * TT, (n + 1) * TT)
                bk = b.bank()

                def mm():
                    for c in range(NCH):
                        src = YNN[:, c, ts] if c < 4 else YNH[:, c - 4, ts]
                        ins = nc.tensor.matmul(b.ps[:, bk, :], w[:, c, :], src, start=(c == 0), stop=(c == NCH - 1))
                    return ins
                b.op(b.pe, mm, reads=[("YNN", c, n) for c in range(4)] + [("YNH", c, n) for c in range(4)] + wkey, writes=[("ps", bk)])
                b.op(b.dve, lambda: nc.vector.tensor_tensor(out=self.X[:, d, ts], in0=self.X[:, d, ts], in1=b.ps[:, bk, :], op=ALU.add),
                     reads=[("ps", bk)], writes=[("X", d, n)])


def _run(inputs, launches, x_override=None):
    inp = {k: np.asarray(v) for k, v in inputs.items()}
    x = inp["x"] if x_override is None else x_override
    Bn = x.shape[0]
    xT = np.ascontiguousarray(np.transpose(x, (0, 2, 1)))
    pT = np.ascontiguousarray(np.transpose(inp["p"], (1, 0, 3, 2)))
    shared = {k: np.ascontiguousarray(v) for k, v in inp.items() if k not in ("x", "p")}
    shared["cpack"] = build_cpack(inp)
    shared["ident"] = np.eye(128, dtype=np.float32)
    shared["btab"] = build_bias_table(inp["na_rpb"])
    hc = host_constants()
    cur = xT
    for phases in launches:
        arrays0 = dict(shared)
        arrays0["xT"] = cur[0]
        arrays0["pT"] = pT[0]
        arrays0.update(hc)
        prog = Prog(None, phases, arrays0)
        nc = prog.build()
        names = list(prog.b.in_arrays.keys())
        in_maps = []
        for ci in range(Bn):
            m = {}
            for nm in names:
                if nm == "xT":
                    m[nm] = cur[ci]
                elif nm == "pT":
                    m[nm] = pT[ci]
                else:
                    m[nm] = prog.b.in_arrays[nm]
            in_maps.append(m)
        res = run_bass_kernel_spmd(nc, in_maps, core_ids=list(range(Bn)))
        cur = np.stack([np.asarray(r["outT"]) for r in res.results], axis=0)
    return np.ascontiguousarray(np.transpose(cur, (0, 2, 1))).astype(np.float32)


def layer_phases(l):
    return [("filt", l), ("ffa", l), ("mixhy", l), ("mixna", l), ("wout", l), ("ffb", l), ("ple", l)]


def kernel(**inputs):
    phases = [("load",)]
    for l in range(DEPTH):
        phases += layer_phases(l)
    phases += [("store", True)]
    return _run(inputs, [phases])
```

```python
import math
import os
import numpy as np
import ml_dtypes
_STOP = int(os.environ.get('FILT_STOP', '9'))
_VAR = os.environ.get('FILT_VAR', '')
_NAS = int(os.environ.get('NA_STOP', '9'))
import concourse.bass as bass
import concourse.mybir as mybir
from concourse.bass_utils import run_bass_kernel_spmd

F32 = mybir.dt.float32
BF16 = mybir.dt.bfloat16
AF = mybir.ActivationFunctionType
ALU = mybir.AluOpType

L = 2048
D = 1024
DFF = 2816
NF = DFF // 128
NCH = D // 128
TT = 512
NT = L // TT
DEPTH = 2
EPS = 1e-6
NHEAD = 8
HY = 512
NFFT = 4096
FILT_ORDER = 64
FILT_EMB = 33
PLE = 256
GRID_W = 64
NROWS = L // GRID_W
NDR = 14

SB_BASE = 20480
OFF_X = 0
OFF_CONST = 65536
CONST_BYTES = 6144
OFF_WRING = OFF_CONST + CONST_BYTES
WSLOT = 4096
NWSLOT = 6
OFF_R = OFF_WRING + WSLOT * NWSLOT
SB_TOP = 225280
R_BYTES = SB_TOP - SB_BASE - OFF_R
OFF_YNHY = 90304

CP = {}
_cp_n = 0


def _cp(name, n):
    global _cp_n
    CP[name] = _cp_n
    _cp_n += n


for _l in range(DEPTH):
    for _nm in ("g_ffa", "g_mix", "g_ffb", "g_ple", "g_out"):
        _cp(f"{_nm}{_l}", 8)
    _cp(f"wsc{_l}", 36)
    _cp(f"bsc{_l}", 12)
    for _nm in ("b_f1", "b_f2", "b_f3", "freq"):
        _cp(f"{_nm}{_l}", 1)
_cp("g_final", 8)
CP_N = _cp_n


def _chunked(v):
    return np.ascontiguousarray(v.reshape(-1, 128).T)


def build_cpack(inp):
    cp = np.zeros((128, CP_N), np.float32)
    for l in range(DEPTH):
        for nm in ("g_ffa", "g_mix", "g_ffb", "g_ple", "g_out"):
            cp[:, CP[f"{nm}{l}"]:CP[f"{nm}{l}"] + 8] = _chunked(inp[nm][l])
        w = inp["w_sc"][l]
        for k in range(3):
            cp[:, CP[f"wsc{l}"] + k * 12: CP[f"wsc{l}"] + (k + 1) * 12] = _chunked(w[k])
        cp[:, CP[f"bsc{l}"]:CP[f"bsc{l}"] + 12] = _chunked(inp["b_sc"][l])
        for nm, src in (("b_f1", "b_f1"), ("b_f2", "b_f2"), ("b_f3", "b_f3"), ("freq", "filt_freq")):
            cp[:64, CP[f"{nm}{l}"]] = inp[src][l]
    cp[:, CP["g_final"]:CP["g_final"] + 8] = _chunked(inp["g_final"])
    return cp


_CONST_CACHE = {}


def host_constants():
    if _CONST_CACHE:
        return _CONST_CACHE
    t = np.arange(L, dtype=np.float64)
    f = np.arange(L, dtype=np.float64)
    ang = 2.0 * np.pi * ((t[:, None] * f[None, :]) % NFFT) / NFFT
    C = np.cos(ang)
    S = np.sin(ang)
    S[:, 0] = np.where((np.arange(L) % 2) == 0, 1.0, -1.0)
    FW = np.stack([C, S], axis=0)
    FW = FW.reshape(2, 16, 128, 16, 128)
    FW = np.ascontiguousarray(FW.transpose(3, 2, 1, 0, 4)).astype(np.float32).astype(ml_dtypes.bfloat16)
    wgt = np.full(L, 2.0 / NFFT)
    wgt[0] = 1.0 / NFFT
    GC = (C * wgt[None, :]).T
    GS = (S * wgt[None, :]).T
    GI = np.stack([GC, GS], axis=0).reshape(2, 16, 128, 4, 512)
    GI = np.ascontiguousarray(GI.transpose(3, 1, 2, 0, 4)).astype(np.float32).astype(ml_dtypes.bfloat16)
    bands = (FILT_EMB - 1) // 2
    tt = np.linspace(0.0, 1.0, L, dtype=np.float32)[:, None]
    w = (2.0 * np.float32(math.pi) * np.arange(L, dtype=np.float32)[:, None] / np.float32(L)).astype(np.float32)
    fb = np.linspace(1e-4, bands - 1, bands, dtype=np.float32)[None, :]
    z = np.concatenate([tt, np.cos(fb * w), -np.sin(fb * w)], axis=-1).astype(np.float32)
    zT = np.ascontiguousarray(z.T)
    max_decay = math.log(1e-2) / 0.3
    min_decay = math.log(1e-2) / 1.5
    deltas = np.abs(np.linspace(min_decay, max_decay, HY, dtype=np.float32))
    decay = np.exp(-tt * deltas[None, :]).astype(np.float32)
    decay = np.ascontiguousarray(decay.reshape(16, 128, HY).transpose(1, 0, 2))
    _CONST_CACHE.update(FW=FW, GI=GI, zT=zT, decay=decay)
    return _CONST_CACHE


def build_bias_table(na_rpb):
    NEG = np.float32(-30000.0)
    p = np.arange(128)
    kcol = p % 64
    krow = p // 64
    q = np.arange(64)
    qcs = np.clip(q - 8, 0, 48)
    mask = (kcol[:, None] >= qcs[None, :]) & (kcol[:, None] < qcs[None, :] + 16)
    dc = np.clip(kcol[:, None] - q[None, :] + 15, 0, 30)
    tab = np.full((DEPTH, NHEAD, 128, NDR, 64), NEG, np.float32)
    for d in range(NDR):
        dr = (d - 7) + krow
        ok = (dr >= -7) & (dr <= 7)
        dri = np.clip(dr + 7, 0, 14)
        vals = na_rpb[:, :, dri[:, None], dc]
        m2 = mask & ok[:, None]
        tab[:, :, :, d, :] = np.where(m2[None, None], vals, NEG)
    return tab


class Sem:
    def __init__(self, nc, name):
        self.h = nc.alloc_semaphore(name)
        self.cnt = 0
        self.name = name


class Eng:
    def __init__(self, nc, e, name, is_pe=False):
        self.e = e
        self.sem = Sem(nc, "s_" + name)
        self.seen = {}
        self.is_pe = is_pe
        self.name = name
        self.last = None

    def wait(self, toks):
        best = {}
        for t in toks:
            if t is None:
                continue
            sem, val = t
            if self.is_pe and sem is self.sem:
                continue
            if self.seen.get(sem, 0) >= val:
                continue
            if best.get(sem, (None, 0))[1] < val:
                best[sem] = (sem, val)
        for sem, val in best.values():
            self.e.wait_ge(sem.h, val)
            self.seen[sem] = val

    def mark(self, ins):
        self.sem.cnt += 1
        ins.then_inc(self.sem.h, 1)
        self.last = (self.sem, self.sem.cnt)
        return self.last


class B:
    def __init__(self):
        nc = bass.Bass("TRN2", target_bir_lowering=False)
        self.nc = nc
        self.pe = Eng(nc, nc.tensor, "pe", is_pe=True)
        self.act = Eng(nc, nc.scalar, "act")
        self.dve = Eng(nc, nc.vector, "dve")
        self.pool = Eng(nc, nc.gpsimd, "pool")
        self.sp = Eng(nc, nc.sync, "sp")
        self.lastw = {}
        self.readers = {}
        self.floor = []
        self.dma_last = {}
        self.sems = {}
        self.dram = {}
        self.in_arrays = {}
        self.sb_cache = {}
        self.ps = nc.alloc_psum_tensor("psum", [128, 8, 512], F32)
        self.ps_i = 0
        self.reserved = set()
        self.uid = 0

    def sb(self, name, shape, dtype, off):
        key = (name, tuple(shape), str(dtype), off)
        if key not in self.sb_cache:
            self.uid += 1
            self.sb_cache[key] = self.nc.alloc_sbuf_tensor_at(f"{name}_{self.uid}", list(shape), dtype,
                                                              offset=SB_BASE + off)
        return self.sb_cache[key]

    def din(self, name, arr):
        if name not in self.dram:
            dt = BF16 if arr.dtype == ml_dtypes.bfloat16 else F32
            self.dram[name] = self.nc.dram_tensor(name, list(arr.shape), dt, kind="ExternalInput").ap()
            self.in_arrays[name] = arr
        return self.dram[name]

    def bank(self):
        for _ in range(16):
            b = self.ps_i % 8
            self.ps_i += 1
            if b not in self.reserved:
                return b
        raise RuntimeError("no free PSUM bank")

    def reserve(self, bk):
        self.reserved.add(bk)

    def unreserve(self, bk):
        self.reserved.discard(bk)

    def getsem(self, name):
        if name not in self.sems:
            self.sems[name] = Sem(self.nc, name)
        return self.sems[name]

    def _deps(self, reads, writes, use_floor=True):
        deps = list(self.floor) if use_floor else []
        for r in reads:
            deps.append(self.lastw.get(r))
        for w in writes:
            deps.append(self.lastw.get(w))
            deps.extend(self.readers.get(w, ()))
        return deps

    def _commit(self, tok, reads, writes):
        for w in writes:
            self.lastw[w] = tok
            self.readers[w] = []
        for r in reads:
            self.readers.setdefault(r, []).append(tok)

    def op(self, eng, fn, reads=(), writes=(), use_floor=True):
        eng.wait(self._deps(reads, writes, use_floor))
        ins = fn()
        tok = eng.mark(ins)
        self._commit(tok, reads, writes)
        return tok

    def dma(self, q, semname, out, in_, reads=(), writes=(), use_floor=True):
        q.wait(self._deps(reads, writes, use_floor))
        sem = self.getsem(semname)
        sem.cnt += 16
        q.e.dma_start(out=out, in_=in_).then_inc(sem.h, 16)
        tok = (sem, sem.cnt)
        if use_floor:
            self.dma_last[sem] = tok
        self._commit(tok, reads, writes)
        return tok

    def barrier(self):
        self.floor = [e.last for e in (self.pe, self.act, self.dve) if e.last is not None]
        self.floor += list(self.dma_last.values())


class WRing:
    def __init__(self, b, plan):
        self.b = b
        self.plan = plan
        self.issued = 0
        self.cur = 0
        self.views = []

    def _issue(self, j):
        b = self.b
        slot = j % NWSLOT
        views = []
        keys = []
        for i, (shape, src, eoff) in enumerate(self.plan[j]):
            v = b.sb("wr", shape, BF16, OFF_WRING + slot * WSLOT + eoff * 2)
            b.dma(b.pool, f"wr{slot}", v[:], src, writes=[("wr", slot, i)], use_floor=False)
            views.append(v)
            keys.append(("wr", slot, i))
        last = b.lastw[keys[-1]]
        for k in keys:
            b.lastw[k] = last
        return views, keys

    def get(self):
        j = self.cur
        self.cur += 1
        while self.issued < min(len(self.plan), j + NWSLOT - 1):
            self.views.append(self._issue(self.issued))
            self.issued += 1
        return self.views[j]

    def start(self):
        pass


class Prog:
    def __init__(self, inp_shapes_only, phases, arrays):
        self.b = B()
        self.A = arrays
        self.phases = phases
        b = self.b
        self.X = b.sb("X", [128, NCH, L], F32, OFF_X)
        o = OFF_CONST
        self.ident = b.sb("ident", [128, 128], BF16, o); o += 256
        self.ones1024 = b.sb("ones1024", [128, 128], BF16, o); o += 256
        self.ones512 = b.sb("ones512", [128, 128], BF16, o); o += 256
        self.onesf = b.sb("onesf", [128, 128], F32, o); o += 512
        self.cp = b.sb("cp", [128, CP_N], F32, o); o += CP_N * 4
        self.epsc = b.sb("epsc", [128, 8], F32, o); o += 32
        assert o <= OFF_CONST + CONST_BYTES, o

    def D(self, name):
        return self.b.din(name, self.A[name])

    def wplan(self, ph):
        kind = ph[0]
        tiles = []
        if kind in ("ffa", "ffb"):
            l = ph[1]
            wg = self.D(f"w_{kind}_gate")[l].rearrange("(k p) n -> p k n", p=128)
            wu = self.D(f"w_{kind}_up")[l].rearrange("(k p) n -> p k n", p=128)
            wd = self.D(f"w_{kind}_down")[l].rearrange("(f p) n -> p f n", p=128)
            for half in range(2):
                for f in range(NF):
                    tiles.append([([128, 8, 128], wg[:, :, f * 128:(f + 1) * 128], 0),
                                  ([128, 8, 128], wu[:, :, f * 128:(f + 1) * 128], 1024)])
                for d in range(NCH):
                    tiles.append([([128, 11, 128], wd[:, 0:11, d * 128:(d + 1) * 128], 0)])
                    tiles.append([([128, 11, 128], wd[:, 11:22, d * 128:(d + 1) * 128], 0)])
        elif kind == "mixhy":
            l = ph[1]
            wi = self.D("w_in")[l].rearrange("(k p) n -> p k n", p=128)
            for j in range(4):
                for s in range(3):
                    c0 = 1536 + s * 512 + j * 128
                    tiles.append([([128, 8, 128], wi[:, :, c0:c0 + 128], 0)])
        elif kind == "mixna":
            l = ph[1]
            wi = self.D("w_in")[l].rearrange("(k p) n -> p k n", p=128)
            for hp in range(4):
                for s in range(3):
                    c0 = s * 512 + hp * 128
                    tiles.append([([128, 8, 128], wi[:, :, c0:c0 + 128], 0)])
        elif kind == "wout":
            l = ph[1]
            wo = self.D("w_out")[l].rearrange("(k p) n -> p k n", p=128)
            for d in range(NCH):
                tiles.append([([128, 8, 128], wo[:, :, d * 128:(d + 1) * 128], 0)])
        elif kind == "ple":
            l = ph[1]
            wg = self.D("w_ple_gate")[l].rearrange("(k p) n -> p k n", p=128)
            wp = self.D("w_ple_proj")[l].rearrange("(k p) n -> p k n", p=128)
            for d in range(NCH):
                tiles.append([([128, 8, 128], wg[:, :, d * 128:(d + 1) * 128], 0),
                              ([128, 2, 128], wp[:, :, d * 128:(d + 1) * 128], 1024)])
        return tiles

    def build(self):
        b = self.b
        nc = b.nc
        plan = []
        for ph in self.phases:
            plan += self.wplan(ph)
        self.W = WRing(b, plan)
        self.W.start()
        b.op(b.dve, lambda: nc.vector.memset(self.ones1024[:], 1.0 / 1024.0), writes=["ones1024"])
        b.op(b.dve, lambda: nc.vector.memset(self.ones512[:], 1.0 / 512.0), writes=["ones512"])
        b.op(b.dve, lambda: nc.vector.memset(self.onesf[:], 1.0), writes=["onesf"])
        b.op(b.dve, lambda: nc.vector.memset(self.epsc[:], EPS), writes=["epsc"])
        b.op(b.dve, lambda: nc.vector.memset(self.epsc[:, 1:2], -math.pi), writes=["epsc"])
        self.KF = nc.dram_tensor("KFscr", [DEPTH, 16, 128, 2 * HY], F32, kind="Internal").ap()
        b.dma(b.sp, "cpl", self.cp[:], self.D("cpack"), writes=["cp"])
        identf = b.sb("identf", [128, 128], F32, OFF_R)
        b.dma(b.sp, "cpl2", identf[:], self.D("ident"), writes=["identf"])
        b.op(b.dve, lambda: nc.vector.tensor_copy(out=self.ident[:], in_=identf[:]), reads=["identf"], writes=["ident"])
        b.barrier()
        for ph in self.phases:
            kind = ph[0]
            if kind == "load":
                self.ph_load()
            elif kind == "filt":
                self.ph_filt(ph[1])
            elif kind in ("ffa", "ffb"):
                self.ph_ffn(kind, ph[1])
            elif kind == "mixhy":
                self.ph_mixhy(ph[1])
            elif kind == "mixna":
                self.ph_mixna(ph[1])
            elif kind == "wout":
                self.ph_wout(ph[1])
            elif kind == "ple":
                self.ph_ple(ph[1])
            elif kind == "store":
                self.ph_store(ph[1])
            elif kind == "zero":
                self.ph_zero(ph[1])
            else:
                raise ValueError(kind)
            b.barrier()
        assert self.W.cur == len(plan), (self.W.cur, len(plan))
        return nc

    def ph_load(self):
        b = self.b
        xT = self.D("xT").rearrange("(c p) t -> p c t", p=128)
        for c in range(NCH):
            b.dma(b.sp, f"xl{c}", self.X[:, c, :], xT[:, c, :], writes=[("X", c, n) for n in range(NT)])

    def ph_store(self, final):
        b = self.b
        nc = b.nc
        oT = nc.dram_tensor("outT", [D, L], F32, kind="ExternalOutput").ap().rearrange("(c p) t -> p c t", p=128)
        if final:
            H32 = b.sb("F32out", [128, NCH, TT], F32, OFF_R + 16384)
            self.norm(("g_final", None), out_fn=None, final_out=(H32, oT))
        else:
            toks = []
            for c in range(NCH):
                toks.append(b.dma(b.sp, f"xo{c}", oT[:, c, :], self.X[:, c, :], reads=[("X", c, n) for n in range(NT)]))
            b.sp.wait(toks)
            return
        b.sp.wait(list(b.dma_last.values()))

    def gcol(self, name, l, c):
        key = f"{name}{l}" if l is not None else name
        return self.cp[:, CP[key] + c: CP[key] + c + 1]

    def rstd(self, out, src, rkeys, wkeys):
        b = self.b
        nc = b.nc
        b.op(b.act, lambda: nc.scalar.activation(out=out, in_=src, func=AF.Sqrt, bias=self.epsc[:, 0:1], scale=1.0),
             reads=list(rkeys) + ["epsc"], writes=wkeys)
        b.op(b.dve, lambda: nc.vector.reciprocal(out=out, in_=out), reads=[], writes=wkeys)

    def norm(self, g, out_fn, final_out=None, nchunks=NCH, src=None):
        b = self.b
        nc = b.nc
        gname, l = g
        SQ = [b.sb("SQ", [128, NCH, TT], BF16, OFF_R + 32768 + i * 8192) for i in range(2)]
        RS = [b.sb("RS", [128, TT], F32, OFF_R + 32768 + 16384 + i * 2048) for i in range(2)]
        H = b.sb("H", [128, NCH, L], BF16, OFF_R)
        for n in range(NT):
            sq = SQ[n % 2]
            rs = RS[n % 2]
            ts = slice(n * TT, (n + 1) * TT)
            b.op(b.act, lambda: nc.scalar.activation(out=sq[:], in_=self.X[:, :, ts], func=AF.Square),
                 reads=[("X", c, n) for c in range(NCH)], writes=[("SQ", n % 2)])
            bk = b.bank()

            def mm():
                for c in range(NCH):
                    ins = nc.tensor.matmul(b.ps[:, bk, :], self.ones1024[:], sq[:, c, :], start=(c == 0), stop=(c == NCH - 1))
                return ins
            b.op(b.pe, mm, reads=[("SQ", n % 2), "ones1024"], writes=[("ps", bk)])
            self.rstd(rs[:], b.ps[:, bk, :], [("ps", bk)], [("RS", n % 2)])
            if final_out is None:
                for c in range(NCH):
                    b.op(b.dve, lambda c=c: nc.vector.scalar_tensor_tensor(
                        out=H[:, c, ts], in0=self.X[:, c, ts], scalar=self.gcol(gname, l, c), in1=rs[:],
                        op0=ALU.mult, op1=ALU.mult),
                        reads=[("X", c, n), ("RS", n % 2), "cp"], writes=[("H", c, n)])
            else:
                ob, oT = final_out
                for c in range(NCH):
                    b.op(b.dve, lambda c=c: nc.vector.scalar_tensor_tensor(
                        out=ob[:, c, :], in0=self.X[:, c, ts], scalar=self.gcol(gname, l, c), in1=rs[:],
                        op0=ALU.mult, op1=ALU.mult),
                        reads=[("X", c, n), ("RS", n % 2), "cp"], writes=[("OB", c)])
                    b.dma(b.sp, f"xo{c}", oT[:, c, ts], ob[:, c, :], reads=[("OB", c)])
        return H

    def ph_ffn(self, kind, l):
        b = self.b
        nc = b.nc
        H = self.norm((f"g_{kind}", l), None)
        A = b.sb("A", [128, NF, 1024], BF16, OFF_R + 32768 + 20480)
        S = [b.sb("S", [128, TT], F32, OFF_R + 32768 + 20480 + 45056 + i * 2048) for i in range(2)]
        assert 32768 + 20480 + 45056 + 4096 <= R_BYTES
        si = 0
        for half in range(2):
            for f in range(NF):
                (wg, wu), wkey = self.W.get()
                for t2 in range(2):
                    n = half * 2 + t2
                    ts = slice(n * TT, (n + 1) * TT)
                    bg = b.bank()
                    bu = b.bank()

                    def mm(w, bk):
                        for c in range(NCH):
                            ins = nc.tensor.matmul(b.ps[:, bk, :], w[:, c, :], H[:, c, ts], start=(c == 0), stop=(c == NCH - 1))
                        return ins
                    rd = [("H", c, n) for c in range(NCH)] + wkey
                    b.op(b.pe, lambda: mm(wg, bg), reads=rd, writes=[("ps", bg)])
                    b.op(b.pe, lambda: mm(wu, bu), reads=rd, writes=[("ps", bu)])
                    s = S[si % 2]
                    b.op(b.act, lambda: nc.scalar.activation(out=s[:], in_=b.ps[:, bg, :], func=AF.Silu),
                         reads=[("ps", bg)], writes=[("S", si % 2)])
                    b.op(b.dve, lambda: nc.vector.tensor_tensor(out=A[:, f, t2 * TT:(t2 + 1) * TT], in0=s[:], in1=b.ps[:, bu, :],
                                                                op=ALU.mult),
                         reads=[("S", si % 2), ("ps", bu)], writes=[("A", f, t2)])
                    si += 1
            for d in range(NCH):
                (w1,), k1 = self.W.get()
                (w2,), k2 = self.W.get()
                for t2 in range(2):
                    n = half * 2 + t2
                    ts = slice(n * TT, (n + 1) * TT)
                    bk = b.bank()

                    def mm():
                        for f in range(NF):
                            w = w1 if f < 11 else w2
                            ins = nc.tensor.matmul(b.ps[:, bk, :], w[:, f % 11, :], A[:, f, t2 * TT:(t2 + 1) * TT],
                                                   start=(f == 0), stop=(f == NF - 1))
                        return ins
                    b.op(b.pe, mm, reads=[("A", f, t2) for f in range(NF)] + k1 + k2, writes=[("ps", bk)])
                    b.op(b.dve, lambda: nc.vector.scalar_tensor_tensor(
                        out=self.X[:, d, ts], in0=b.ps[:, bk, :], scalar=0.5, in1=self.X[:, d, ts],
                        op0=ALU.mult, op1=ALU.add),
                        reads=[("ps", bk)], writes=[("X", d, n)])

    def ph_ple(self, l):
        b = self.b
        nc = b.nc
        H = self.norm(("g_ple", l), None)
        pT = b.sb("pT", [128, 2, L], BF16, OFF_R + 32768 + 20480)
        SG = [b.sb("SG", [128, TT], F32, OFF_R + 32768 + 20480 + 8192 + i * 2048) for i in range(2)]
        pd = self.D("pT")[l].rearrange("(k p) t -> p k t", p=128)
        b.dma(b.pool, "pTl", pT[:], pd, writes=["pT"])
        si = 0
        for d in range(NCH):
            (wg, wp), wkey = self.W.get()
            for n in range(NT):
                ts = slice(n * TT, (n + 1) * TT)
                bg = b.bank()
                bp = b.bank()

                def mmg():
                    for c in range(NCH):
                        ins = nc.tensor.matmul(b.ps[:, bg, :], wg[:, c, :], H[:, c, ts], start=(c == 0), stop=(c == NCH - 1))
                    return ins

                def mmp():
                    for c in range(2):
                        ins = nc.tensor.matmul(b.ps[:, bp, :], wp[:, c, :], pT[:, c, ts], start=(c == 0), stop=(c == 1))
                    return ins
                b.op(b.pe, mmg, reads=[("H", c, n) for c in range(NCH)] + wkey, writes=[("ps", bg)])
                b.op(b.pe, mmp, reads=["pT"] + wkey, writes=[("ps", bp)])
                sg = SG[si % 2]
                b.op(b.act, lambda: nc.scalar.activation(out=sg[:], in_=b.ps[:, bg, :], func=AF.Sigmoid),
                     reads=[("ps", bg)], writes=[("SG", si % 2)])
                b.op(b.dve, lambda: nc.vector.tensor_tensor(out=sg[:], in0=sg[:], in1=b.ps[:, bp, :], op=ALU.mult),
                     reads=[("ps", bp)], writes=[("SG", si % 2)])
                b.op(b.dve, lambda: nc.vector.tensor_tensor(out=self.X[:, d, ts], in0=self.X[:, d, ts], in1=sg[:], op=ALU.add),
                     reads=[("SG", si % 2)], writes=[("X", d, n)])
                si += 1

    def ph_filt(self, l):
        b = self.b
        nc = b.nc
        R0 = OFF_R
        zT = b.sb("zT", [128, L], F32, R0 + 0)
        hbuf = [b.sb("hbuf", [128, L], F32, R0 + 8192 + i * 8192) for i in range(2)]
        ARG = [b.sb("ARG", [128, TT], F32, R0 + 24576 + i * 2048) for i in range(2)]
        wf = [b.sb("wf1", [128, 64], F32, R0 + 28672), b.sb("wf2", [128, 64], F32, R0 + 28928),
              b.sb("wf3", [128, 64], F32, R0 + 29184)]
        wf4 = b.sb("wf4", [128, 1024], BF16, R0 + 29440)
        H3B = b.sb("H3B", [128, L], BF16, R0 + 92160)
        fbt = b.sb("fbt", [128, 4], F32, R0 + 33536)
        MSK = [b.sb("MSK", [128, TT], F32, R0 + 34816)] * 2
        hs = b.sb("hs", [128, 16, HY], BF16, R0 + 36864)
        hd = b.sb("hd", [128, 16, HY], BF16, R0 + 53248)
        KO = [b.sb("KO", [128, 2, HY], F32, R0 + 77824 + i * 4096) for i in range(2)]
        inv = b.sb("inv", [128, HY], F32, R0 + 86016)
        biasb = b.sb("biasb", [128, HY], F32, R0 + 88064)
        brow = b.sb("brow", [128, HY], F32, R0 + 90112)
        FWT = [b.sb("FWT", [128, 16, 2, 128], BF16, R0 + 92160 + i * 8192) for i in range(2)]
        assert 92160 + 16384 <= R_BYTES
        KF = self.KF
        b.dma(b.sp, "fl0", zT[0:FILT_EMB, :], self.D("zT"), writes=["zT"])
        b.dma(b.sp, "fl1", wf[0][0:FILT_EMB, :], self.D("w_f1")[l], writes=["wf0"])
        b.dma(b.sp, "fl2", wf[1][0:64, :], self.D("w_f2")[l], writes=["wf1"])
        b.dma(b.sp, "fl3", wf[2][0:64, :], self.D("w_f3")[l], writes=["wf2"])
        b.dma(b.pool, "fl4", wf4[0:64, :], self.D("w_f4")[l], writes=["wf4"])
        b.dma(b.sp, "fl5", brow[0:1, :], self.D("hy_bias")[l:l + 1, :], writes=["brow"])
        fq = self.cp[0:64, CP[f"freq{l}"]:CP[f"freq{l}"] + 1]
        for i in range(3):
            bi = self.cp[0:64, CP[f"b_f{i + 1}{l}"]:CP[f"b_f{i + 1}{l}"] + 1]
            b.op(b.dve, lambda: nc.vector.tensor_tensor(out=fbt[0:64, i:i + 1], in0=bi, in1=fq, op=ALU.mult),
                 reads=["cp"], writes=[("fbt", i)])
        if _STOP <= 1:
            return
        hin = zT
        kin = FILT_EMB
        ai = 0
        for i in range(3):
            hout = hbuf[i % 2]
            for n in range(NT):
                ts = slice(n * TT, (n + 1) * TT)
                bk = b.bank()
                b.op(b.pe, lambda: nc.tensor.matmul(b.ps[0:64, bk, :], wf[i][0:kin, :], hin[0:kin, ts], start=True, stop=True),
                     reads=[f"wf{i}", ("hin", i, n)] + (["zT"] if i == 0 else []), writes=[("ps", bk)])
                arg = ARG[ai % 2]
                b.op(b.dve, lambda: nc.vector.tensor_scalar(out=arg[0:64, :], in0=b.ps[0:64, bk, :], scalar1=fq,
                                                            scalar2=fbt[0:64, i:i + 1], op0=ALU.mult, op1=ALU.add),
                     reads=[("ps", bk), ("fbt", i), "cp"], writes=[("ARG", ai % 2)])
                msk = MSK[ai % 2]
                b.op(b.dve, lambda: nc.vector.tensor_single_scalar(out=msk[0:64, :], in_=arg[0:64, :], scalar=math.pi, op=ALU.is_gt),
                     reads=[("ARG", ai % 2)], writes=["MSK"])
                b.op(b.dve, lambda: nc.vector.scalar_tensor_tensor(out=arg[0:64, :], in0=msk[0:64, :], scalar=-2.0 * math.pi,
                                                                   in1=arg[0:64, :], op0=ALU.mult, op1=ALU.add),
                     reads=["MSK"], writes=[("ARG", ai % 2)])
                b.op(b.dve, lambda: nc.vector.tensor_single_scalar(out=msk[0:64, :], in_=arg[0:64, :], scalar=-math.pi, op=ALU.is_lt),
                     reads=[("ARG", ai % 2)], writes=["MSK"])
                b.op(b.dve, lambda: nc.vector.scalar_tensor_tensor(out=arg[0:64, :], in0=msk[0:64, :], scalar=2.0 * math.pi,
                                                                   in1=arg[0:64, :], op0=ALU.mult, op1=ALU.add),
                     reads=["MSK"], writes=[("ARG", ai % 2)])
                if i < 2:
                    b.op(b.act, lambda: nc.scalar.activation(out=hout[0:64, ts], in_=arg[0:64, :], func=AF.Sin),
                         reads=[("ARG", ai % 2)], writes=[("hin", i + 1, n)])
                else:
                    b.op(b.act, lambda: nc.scalar.activation(out=H3B[0:64, ts], in_=arg[0:64, :], func=AF.Sin),
                         reads=[("ARG", ai % 2)], writes=[("h3b", n)])
                ai += 1
            hin = hout
            kin = 64
        h3b = H3B
        b.barrier()
        HF = [b.sb("HF", [128, 2, HY], F32, R0 + 0 + i * 4096) for i in range(3)]
        HA = [b.sb("HA", [128, 2, HY], BF16, R0 + 12288 + i * 2048) for i in range(3)]
        DEC = [b.sb("DEC", [128, HY], F32, R0 + 18432 + i * 2048) for i in range(3)]
        babs = b.bank()
        b.reserve(babs)
        dec = self.D("decay")
        for j in range(16):
            q3 = j % 3
            bf_ = b.bank()
            bb_ = b.bank()
            b.op(b.pe, lambda: nc.tensor.matmul(b.ps[:, bf_, :], h3b[0:64, j * 128:(j + 1) * 128], wf4[0:64, 0:HY], start=True, stop=True),
                 reads=[("h3b", j // 4), "wf4"], writes=[("ps", bf_)])
            b.op(b.pe, lambda: nc.tensor.matmul(b.ps[:, bb_, :], h3b[0:64, j * 128:(j + 1) * 128], wf4[0:64, HY:2 * HY], start=True, stop=True),
                 reads=[("h3b", j // 4), "wf4"], writes=[("ps", bb_)])
            dc = DEC[q3]
            b.dma(b.sp, f"dec{q3}", dc[:], dec[:, j, :], writes=[("DEC", q3)])
            hf = HF[q3]
            ha = HA[q3]
            b.op(b.dve, lambda: nc.vector.tensor_tensor(out=hf[:, 0, :], in0=b.ps[:, bf_, :], in1=dc[:], op=ALU.mult),
                 reads=[("ps", bf_), ("DEC", q3)], writes=[("HF", q3)])
            b.op(b.dve, lambda: nc.vector.tensor_tensor(out=hf[:, 1, :], in0=b.ps[:, bb_, :], in1=dc[:], op=ALU.mult),
                 reads=[("ps", bb_), ("DEC", q3)], writes=[("HF", q3)])
            if j == 0:
                b.op(b.dve, lambda: nc.vector.tensor_tensor(out=hf[0:1, 0, :], in0=hf[0:1, 0, :], in1=hf[0:1, 1, :], op=ALU.add),
                     reads=[], writes=[("HF", q3)])
                b.op(b.dve, lambda: nc.vector.memset(hf[0:1, 1, :], 0.0), reads=[], writes=[("HF", q3)])
            b.op(b.act, lambda: nc.scalar.activation(out=ha[:], in_=hf[:], func=AF.Abs),
                 reads=[("HF", q3)], writes=[("HA", q3)])
            b.op(b.dve, lambda: nc.vector.tensor_tensor(out=hs[:, j, :], in0=hf[:, 0, :], in1=hf[:, 1, :], op=ALU.add),
                 reads=[("HF", q3)], writes=[("hs", j)])
            b.op(b.dve, lambda: nc.vector.tensor_tensor(out=hd[:, j, :], in0=hf[:, 0, :], in1=hf[:, 1, :], op=ALU.subtract),
                 reads=[("HF", q3)], writes=[("hd", j)])

            def mmabs():
                nc.tensor.matmul(b.ps[:, babs, :], self.ones1024[:], ha[:, 0, :], start=(j == 0), stop=False)
                return nc.tensor.matmul(b.ps[:, babs, :], self.ones1024[:], ha[:, 1, :], start=False, stop=(j == 15))
            b.op(b.pe, mmabs, reads=[("HA", q3), "ones1024"], writes=[("ps", babs)])
        b.op(b.dve, lambda: nc.vector.reciprocal(out=inv[:], in_=b.ps[:, babs, :]), reads=[("ps", babs)], writes=["inv"])
        b.op(b.dve, lambda: nc.vector.tensor_scalar(out=inv[:], in0=inv[:], scalar1=1.0 / 1024.0, scalar2=None, op0=ALU.mult),
             reads=[], writes=["inv"])
        b.unreserve(babs)
        bk = b.bank()
        b.op(b.pe, lambda: nc.tensor.matmul(b.ps[:, bk, :], self.onesf[0:1, :], brow[0:1, :], start=True, stop=True),
             reads=["brow", "onesf"], writes=[("ps", bk)])
        b.op(b.dve, lambda: nc.vector.tensor_copy(out=biasb[:], in_=b.ps[:, bk, :]), reads=[("ps", bk)], writes=["biasb"])
        b.barrier()
        if _STOP <= 4:
            return
        fw = self.D("FW")

        def load_fw(mm_):
            b.dma(b.sp, f"fwt{mm_ % 2}", FWT[mm_ % 2][:].rearrange("p a c f -> p (a c f)"), fw[mm_].rearrange("p a c f -> p (a c f)"),
                  writes=[("FWT", mm_ % 2)])
        load_fw(0)
        for m in range(16):
            fwt = FWT[m % 2]
            if m + 1 < 16:
                load_fw(m + 1)
            bK = b.bank()
            bQ = b.bank()

            def mmk(cs, bk_, rhs):
                for kc in range(16):
                    ins = nc.tensor.matmul(b.ps[:, bk_, :], fwt[:, kc, cs, :], rhs[:, kc, :], start=(kc == 0), stop=(kc == 15))
                return ins
            b.op(b.pe, lambda: mmk(0, bK, hs), reads=[("FWT", m % 2)] + [("hs", j) for j in range(16)], writes=[("ps", bK)])
            b.op(b.pe, lambda: mmk(1, bQ, hd), reads=[("FWT", m % 2)] + [("hd", j) for j in range(16)], writes=[("ps", bQ)])
            ko = KO[m % 2]
            b.op(b.dve, lambda: nc.vector.tensor_tensor(out=ko[:, 0, :], in0=b.ps[:, bK, :], in1=inv[:], op=ALU.mult),
                 reads=[("ps", bK), "inv"], writes=[("KO", m % 2)])
            b.op(b.dve, lambda: nc.vector.tensor_tensor(out=ko[:, 0, :], in0=ko[:, 0, :], in1=biasb[:], op=ALU.add),
                 reads=["biasb"], writes=[("KO", m % 2)])
            b.op(b.dve, lambda: nc.vector.tensor_tensor(out=ko[:, 1, :], in0=b.ps[:, bQ, :], in1=inv[:], op=ALU.mult),
                 reads=[("ps", bQ), "inv"], writes=[("KO", m % 2)])
            if m == 0 and 'nonyq' not in _VAR:
                bN = b.bank()

                def mmn():
                    for kc in range(16):
                        ins = nc.tensor.matmul(b.ps[0:1, bN, :], fwt[:, kc, 1, 0:1], hs[:, kc, :], start=(kc == 0), stop=(kc == 15))
                    return ins
                b.op(b.pe, mmn, reads=[("FWT", m % 2)] + [("hs", j) for j in range(16)], writes=[("ps", bN)])
                b.op(b.dve, lambda: nc.vector.tensor_tensor(out=ko[0:1, 1, :], in0=b.ps[0:1, bN, :], in1=inv[0:1, :], op=ALU.mult),
                     reads=[("ps", bN), "inv"], writes=[("KO", m % 2)])
                b.op(b.dve, lambda: nc.vector.tensor_tensor(out=ko[0:1, 1, :], in0=ko[0:1, 1, :], in1=biasb[0:1, :], op=ALU.add),
                     reads=["biasb"], writes=[("KO", m % 2)])
            if 'nostore' not in _VAR:
                b.dma(b.sp, f"kfo{m % 2}", KF[l, m], ko[:].rearrange("p a c -> p (a c)"), reads=[("KO", m % 2)], writes=[("KF", l, m)])

    def ph_mixhy(self, l):
        b = self.b
        nc = b.nc
        R0 = OFF_R
        H = self.norm(("g_mix", l), None)
        b.barrier()
        U = [b.sb("U", [128, L + 2], F32, R0 + 32768 + i * 8256) for i in range(3)]
        T = [b.sb("T", [128, L], F32, R0 + 57536 + i * 8192) for i in range(2)]
        vxf = b.sb("vxf", [128, L], BF16, R0 + 32768 + 64)
        sx0 = b.sb("sx0", [128, 4, L], BF16, R0 + 73920)
        vxT = b.sb("vxT", [128, 16, HY], BF16, R0 + OFF_YNHY)
        assert OFF_YNHY + 16384 <= R_BYTES
        for i in range(3):
            b.op(b.dve, lambda: nc.vector.memset(U[i][:, 0:1], 0.0), writes=[("Uh", i)])
            b.op(b.dve, lambda: nc.vector.memset(U[i][:, L + 1:L + 2], 0.0), writes=[("Uh", i)])
        wsc = CP[f"wsc{l}"]
        bsc = CP[f"bsc{l}"]
        for j in range(4):
            for s_ in range(3):
                (w,), wkey = self.W.get()
                for n in range(NT):
                    ts = slice(n * TT, (n + 1) * TT)
                    bk = b.bank()

                    def mm():
                        for c in range(NCH):
                            ins = nc.tensor.matmul(b.ps[:, bk, :], w[:, c, :], H[:, c, ts], start=(c == 0), stop=(c == NCH - 1))
                        return ins
                    b.op(b.pe, mm, reads=[("H", c, n) for c in range(NCH)] + wkey, writes=[("ps", bk)])
                    b.op(b.act, lambda: nc.scalar.copy(out=U[s_][:, 1 + n * TT:1 + (n + 1) * TT], in_=b.ps[:, bk, :]),
                         reads=[("ps", bk)], writes=[("U", s_, n)])
                cc = s_ * 4 + j
                w0 = self.cp[:, wsc + cc:wsc + cc + 1]
                w1 = self.cp[:, wsc + 12 + cc:wsc + 12 + cc + 1]
                w2 = self.cp[:, wsc + 24 + cc:wsc + 24 + cc + 1]
                bb = self.cp[:, bsc + cc:bsc + cc + 1]
                tt_ = T[0] if s_ < 2 else T[1]
                ukeys = [("U", s_, n) for n in range(NT)] + [("Uh", s_)]
                b.op(b.act, lambda: nc.scalar.activation(out=tt_[:], in_=U[s_][:, 1:L + 1], func=AF.Identity, bias=bb, scale=w1),
                     reads=ukeys + ["cp"], writes=[("T", 0 if s_ < 2 else 1)])
                b.op(b.dve, lambda: nc.vector.scalar_tensor_tensor(out=tt_[:], in0=U[s_][:, 0:L], scalar=w0, in1=tt_[:],
                                                                   op0=ALU.mult, op1=ALU.add),
                     reads=ukeys + ["cp"], writes=[("T", 0 if s_ < 2 else 1)])
                dst = sx0[:, j, :] if s_ == 0 else tt_[:]
                dkeys = [("sx0", j)] if s_ == 0 else []
                b.op(b.dve, lambda: nc.vector.scalar_tensor_tensor(out=dst, in0=U[s_][:, 2:L + 2], scalar=w2, in1=tt_[:],
                                                                   op0=ALU.mult, op1=ALU.add),
                     reads=ukeys + ["cp"], writes=[("T", 0 if s_ < 2 else 1)] + dkeys)
            b.op(b.dve, lambda: nc.vector.tensor_tensor(out=vxf[:], in0=T[0][:], in1=T[1][:], op=ALU.mult),
                 reads=[("T", 0), ("T", 1)], writes=["vxf"] + [("U", 0, n) for n in range(NT)] + [("Uh", 0)])
            for tq in range(4):
                bk = b.bank()

                def mmt():
                    for i in range(4):
                        tc_ = tq * 4 + i
                        ins = nc.tensor.matmul(b.ps[:, bk, i * 128:(i + 1) * 128], vxf[:, tc_ * 128:(tc_ + 1) * 128], self.ident[:],
                                               start=True, stop=True)
                    return ins
                b.op(b.pe, mmt, reads=["vxf", "ident"], writes=[("ps", bk)])
                b.op(b.act, lambda: nc.scalar.copy(out=vxT[:, tq * 4:(tq + 1) * 4, j * 128:(j + 1) * 128],
                                                   in_=b.ps[:, bk, :].rearrange("p (a c) -> p a c", a=4)),
                     reads=[("ps", bk)], writes=[("vxT", tq, j)])
        b.barrier()
        YF = b.sb("YF", [128, 16, 2, HY], BF16, R0 + 0)
        FWT = [b.sb("FWT", [128, 16, 2, 128], BF16, R0 + 32768 + i * 8192) for i in range(2)]
        KFT = [b.sb("KFT", [128, 2, HY], F32, R0 + 49152 + i * 4096) for i in range(2)]
        TMP = [b.sb("TMP", [128, 2, HY], F32, R0 + 57344 + i * 4096) for i in range(2)]
        fw = self.D("FW")
        KF = self.KF
        vkeys = [("vxT", tq, j) for tq in range(4) for j in range(4)]
        for m in range(16):
            fwt = FWT[m % 2]
            b.dma(b.sp, f"fwt{m % 2}", fwt[:].rearrange("p a c f -> p (a c f)"), fw[m].rearrange("p a c f -> p (a c f)"), writes=[("FWT", m % 2)])
            kft = KFT[m % 2]
            b.dma(b.sp, f"kfi{m % 2}", kft[:].rearrange("p a c -> p (a c)"), KF[l, m], reads=[("KF", l, m)], writes=[("KFT", m % 2)])
            bV = b.bank()
            bW = b.bank()

            def mmk(cs, bk_):
                for kc in range(16):
                    ins = nc.tensor.matmul(b.ps[:, bk_, :], fwt[:, kc, cs, :], vxT[:, kc, :], start=(kc == 0), stop=(kc == 15))
                return ins
            b.op(b.pe, lambda: mmk(0, bV), reads=[("FWT", m % 2)] + vkeys, writes=[("ps", bV)])
            b.op(b.pe, lambda: mmk(1, bW), reads=[("FWT", m % 2)] + vkeys, writes=[("ps", bW)])
            tmp = TMP[m % 2]
            kk = [("KFT", m % 2)]
            b.op(b.dve, lambda: nc.vector.tensor_tensor(out=tmp[:, 0, :], in0=b.ps[:, bV, :], in1=kft[:, 0, :], op=ALU.mult),
                 reads=[("ps", bV)] + kk, writes=[("TMP", m % 2, 0)])
            b.op(b.dve, lambda: nc.vector.tensor_tensor(out=tmp[:, 1, :], in0=b.ps[:, bW, :], in1=kft[:, 1, :], op=ALU.mult),
                 reads=[("ps", bW)] + kk, writes=[("TMP", m % 2, 1)])
            b.op(b.dve, lambda: nc.vector.tensor_tensor(out=YF[:, m, 0, :], in0=tmp[:, 0, :], in1=tmp[:, 1, :], op=ALU.subtract),
                 reads=[("TMP", m % 2, 0), ("TMP", m % 2, 1)], writes=[("YF", m, 0)])
            b.op(b.dve, lambda: nc.vector.tensor_tensor(out=tmp[:, 0, :], in0=b.ps[:, bV, :], in1=kft[:, 1, :], op=ALU.mult),
                 reads=[("ps", bV)] + kk, writes=[("TMP", m % 2, 0)])
            b.op(b.dve, lambda: nc.vector.tensor_tensor(out=tmp[:, 1, :], in0=b.ps[:, bW, :], in1=kft[:, 0, :], op=ALU.mult),
                 reads=[("ps", bW)] + kk, writes=[("TMP", m % 2, 1)])
            b.op(b.dve, lambda: nc.vector.tensor_tensor(out=YF[:, m, 1, :], in0=tmp[:, 0, :], in1=tmp[:, 1, :], op=ALU.add),
                 reads=[("TMP", m % 2, 0), ("TMP", m % 2, 1)], writes=[("YF", m, 1)])
            if m == 0:
                b.op(b.dve, lambda: nc.vector.tensor_tensor(out=YF[0:1, 0, 0, :], in0=b.ps[0:1, bV, :], in1=kft[0:1, 0, :], op=ALU.mult),
                     reads=[("ps", bV)] + kk, writes=[("YF", m, 0)])
                b.op(b.dve, lambda: nc.vector.tensor_tensor(out=YF[0:1, 0, 1, :], in0=b.ps[0:1, bW, :], in1=kft[0:1, 1, :], op=ALU.mult),
                     reads=[("ps", bW)] + kk, writes=[("YF", m, 1)])
        b.barrier()
        GR = [b.sb("GR", [128, 2, TT], BF16, R0 + 32768 + i * 2048) for i in range(4)]
        zb = b.sb("zb", [128, 4, TT], F32, R0 + 40960)
        sqz = b.sb("sqz", [128, 4, TT], BF16, R0 + 49152)
        rs2 = b.sb("rs2", [128, TT], F32, R0 + 53248)
        YNH = b.sb("YNH", [128, 4, L], BF16, R0 + OFF_YNHY)
        gi = self.D("GI")
        yfk = [("YF", m, ri) for m in range(16) for ri in range(2)]
        gidx = 0
        for n in range(NT):
            ts = slice(n * TT, (n + 1) * TT)
            bks = [b.bank() for _ in range(4)]
            for bk in bks:
                b.reserve(bk)
            for m in range(16):
                g = GR[gidx % 4]
                b.dma(b.sp, f"gr{gidx % 4}", g[:].rearrange("p a c -> p (a c)"), gi[n, m].rearrange("p a c -> p (a c)"), writes=[("GR", gidx % 4)])

                def mmi():
                    for ri in range(2):
                        for j in range(4):
                            ins = nc.tensor.matmul(b.ps[:, bks[j], :], YF[:, m, ri, j * 128:(j + 1) * 128], g[:, ri, :],
                                                   start=(m == 0 and ri == 0), stop=(m == 15 and ri == 1))
                    return ins
                b.op(b.pe, mmi, reads=[("GR", gidx % 4)] + (yfk if m == 0 else []), writes=[("ps", bk) for bk in bks])
                gidx += 1
            for j in range(4):
                b.op(b.dve, lambda: nc.vector.tensor_tensor(out=zb[:, j, :], in0=b.ps[:, bks[j], :], in1=sx0[:, j, ts], op=ALU.mult),
                     reads=[("ps", bks[j]), ("sx0", j)], writes=[("zb", j)])
                b.unreserve(bks[j])
            b.op(b.act, lambda: nc.scalar.activation(out=sqz[:], in_=zb[:], func=AF.Square),
                 reads=[("zb", j) for j in range(4)], writes=["sqz"])
            bk = b.bank()

            def mms():
                for j in range(4):
                    ins = nc.tensor.matmul(b.ps[:, bk, :], self.ones512[:], sqz[:, j, :], start=(j == 0), stop=(j == 3))
                return ins
            b.op(b.pe, mms, reads=["sqz", "ones512"], writes=[("ps", bk)])
            self.rstd(rs2[:], b.ps[:, bk, :], [("ps", bk)], ["rs2"])
            for j in range(4):
                b.op(b.dve, lambda: nc.vector.scalar_tensor_tensor(out=YNH[:, j, ts], in0=zb[:, j, :], scalar=self.gcol("g_out", l, 4 + j),
                                                                   in1=rs2[:], op0=ALU.mult, op1=ALU.mult),
                     reads=[("zb", j), "rs2", "cp"], writes=[("YNH", j, n)])

    def ph_mixna(self, l):
        b = self.b
        nc = b.nc
        R0 = OFF_R
        H = self.norm(("g_mix", l), None)
        b.barrier()
        qT = b.sb("qT", [128, L], BF16, R0 + 32768)
        kT = b.sb("kT", [128, L], BF16, R0 + 36864)
        vE = b.sb("vE", [128, 16, 2, 65], BF16, R0 + 40960)
        vO = b.sb("vO", [128, 15, 2, 65], BF16, R0 + 45120)
        yna = b.sb("yna", [128, 16, 512], BF16, R0 + 49152)
        E = b.sb("E", [128, NHEAD, NDR, 64], BF16, R0 + 65536)
        BST = [b.sb("BST", [128, NDR * 64], F32, R0 + 79872)] * 2
        PEX = [b.sb("PEX", [128, 512], BF16, R0 + 79872 + i * 1024) for i in range(4)]
        PP = [b.sb("PP", [128, 512], BF16, R0 + 83968 + i * 1024) for i in range(4)]
        REC = [b.sb("REC", [128, 2], F32, R0 + 88064 + i * 32) for i in range(2)]
        ss = b.sb("ss", [128, 16], F32, R0 + 88128)
        rs16 = b.sb("rs16", [128, 16], F32, R0 + 88192)
        YT = [b.sb("YT", [128, 512], BF16, R0 + 83968 + i * 1024) for i in range(2)]
        junk = b.sb("junk", [128, 512], F32, R0 + 79872)
        assert 88256 <= OFF_YNHY
        tab = self.D("btab")
        for h in range(NHEAD):
            bst = BST[h % 2]
            b.dma(b.sp, "bst", bst[:], tab[l, h].rearrange("p d q -> p (d q)"), writes=["BST"])
            b.op(b.act, lambda: nc.scalar.activation(out=E[:, h, :, :].rearrange("p d q -> p (d q)"), in_=bst[:], func=AF.Exp),
                 reads=["BST"], writes=[("E", h)])
        b.barrier()
        b.op(b.dve, lambda: nc.vector.memset(vE[:, :, :, 64:65], 1.0), writes=["vE1"])
        b.op(b.dve, lambda: nc.vector.memset(vO[:, :, :, 64:65], 1.0), writes=["vO1"])
        pi = 0
        for hp in range(4):
            (wq,), kq = self.W.get()
            (wk,), kk_ = self.W.get()
            for (w, wkey, dst, dname) in ((wq, kq, qT, "qT"), (wk, kk_, kT, "kT")):
                for n in range(NT):
                    ts = slice(n * TT, (n + 1) * TT)
                    bk = b.bank()

                    def mm():
                        for c in range(NCH):
                            ins = nc.tensor.matmul(b.ps[:, bk, :], w[:, c, :], H[:, c, ts], start=(c == 0), stop=(c == NCH - 1))
                        return ins
                    b.op(b.pe, mm, reads=[("H", c, n) for c in range(NCH)] + wkey, writes=[("ps", bk)])
                    if dname == "qT":
                        b.op(b.act, lambda: nc.scalar.copy(out=dst[:, ts], in_=b.ps[:, bk, :]), reads=[("ps", bk)], writes=[(dname, n)])
                    else:
                        b.op(b.dve, lambda: nc.vector.tensor_copy(out=dst[:, ts], in_=b.ps[:, bk, :]), reads=[("ps", bk)], writes=[(dname, n)])
            (wv,), kv = self.W.get()
            for (vt, ntile, toff, vname) in ((vE, 16, 0, "vE"), (vO, 15, 64, "vO")):
                for g0 in range(0, ntile, 4):
                    cnt = min(4, ntile - g0)
                    bk = b.bank()

                    def mmv():
                        for i in range(cnt):
                            t0 = toff + (g0 + i) * 128
                            for c in range(NCH):
                                ins = nc.tensor.matmul(b.ps[:, bk, i * 128:(i + 1) * 128], H[:, c, t0:t0 + 128], wv[:, c, :],
                                                       start=(c == 0), stop=(c == NCH - 1))
                        return ins
                    nn = sorted(set([(toff + (g0 + i) * 128) // TT for i in range(cnt)] + [(toff + (g0 + i) * 128 + 127) // TT for i in range(cnt)]))
                    b.op(b.pe, mmv, reads=[("H", c, n) for c in range(NCH) for n in nn] + kv, writes=[("ps", bk)])
                    b.op(b.act, lambda: nc.scalar.copy(out=vt[:, g0:g0 + cnt, :, 0:64],
                                                       in_=b.ps[:, bk, 0:cnt * 128].rearrange("p (a h d) -> p a h d", a=cnt, h=2)),
                         reads=[("ps", bk)], writes=[(vname, g0 // 4), vname + "1"])
            vkeys = {"vE": [("vE", g) for g in range(4)] + ["vE1"], "vO": [("vO", g) for g in range(4)] + ["vO1"]}
            for jp in range(NROWS // 2 if _NAS > 1 else 0):
                rows = (2 * jp, 2 * jp + 1)
                info = []
                for r in rows:
                    rs_ = min(max(r - 4, 0), NROWS - 8)
                    if rs_ % 2 == 0:
                        info.append((r, rs_ * 64, (rs_ - r) + 7, vE, "vE", rs_ // 2))
                    else:
                        info.append((r, rs_ * 64, (rs_ - r) + 7, vO, "vO", (rs_ - 1) // 2))
                bs = [b.bank(), b.bank()]
                for hh in range(2):
                    def mms():
                        for ri, (r, ktok, d0, vt, vn, jv0) in enumerate(info):
                            for kc in range(4):
                                ins = nc.tensor.matmul(b.ps[:, bs[hh], ri * 256 + kc * 64: ri * 256 + (kc + 1) * 64],
                                                       kT[hh * 64:(hh + 1) * 64, ktok + kc * 128: ktok + (kc + 1) * 128],
                                                       qT[hh * 64:(hh + 1) * 64, r * 64:(r + 1) * 64], start=True, stop=True)
                        return ins
                    b.op(b.pe, mms, reads=[("qT", n) for n in range(NT)] + [("kT", n) for n in range(NT)], writes=[("ps", bs[hh])])
                pbuf = []
                for hh in range(2):
                    pex = PEX[pi % 4]
                    pp = PP[pi % 4]
                    b.op(b.act, lambda: nc.scalar.activation(out=pex[:], in_=b.ps[:, bs[hh], :], func=AF.Exp, scale=0.125),
                         reads=[("ps", bs[hh])], writes=[("PEX", pi % 4)])
                    if _NAS > 2:
                        for ri, (r, ktok, d0, vt, vn, jv0) in enumerate(info):
                            b.op(b.dve, lambda: nc.vector.tensor_tensor(
                                out=pp[:, ri * 256:(ri + 1) * 256].rearrange("p (k q) -> p k q", k=4),
                                in0=pex[:, ri * 256:(ri + 1) * 256].rearrange("p (k q) -> p k q", k=4),
                                in1=E[:, 2 * hp + hh, d0:d0 + 7:2, :], op=ALU.mult),
                                reads=[("PEX", pi % 4), ("E", 2 * hp + hh)], writes=[("PP", pi % 4, ri)])
                    pbuf.append((pp, pi % 4))
                    pi += 1
                if _NAS <= 3:
                    continue
                bo = [b.bank(), b.bank()]
                for ri, (r, ktok, d0, vt, vn, jv0) in enumerate(info):
                    po = ri * 64

                    def mmo():
                        for hh in range(2):
                            pp, pidx = pbuf[hh]
                            for kc in range(4):
                                ins = nc.tensor.matmul(b.ps[po:po + 64, bo[ri], hh * 65:(hh + 1) * 65],
                                                       pp[:, ri * 256 + kc * 64: ri * 256 + (kc + 1) * 64],
                                                       vt[:, jv0 + kc, hh, :], start=(kc == 0), stop=(kc == 3))
                        return ins
                    b.op(b.pe, mmo, reads=[("PP", pbuf[0][1], ri), ("PP", pbuf[1][1], ri)] + vkeys[vn], writes=[("ps", bo[ri])])
                if _NAS <= 4:
                    continue
                rec = REC[jp % 2]
                for ri in range(2):
                    po = ri * 64
                    b.op(b.dve, lambda: nc.vector.reciprocal(out=rec[po:po + 64, :], in_=b.ps[po:po + 64, bo[ri], 64:130:65]),
                         reads=[("ps", bo[ri])], writes=[("REC", jp % 2, ri)])
                    for hh in range(2):
                        b.op(b.dve, lambda: nc.vector.tensor_scalar(out=yna[po:po + 64, jp, hp * 128 + hh * 64: hp * 128 + (hh + 1) * 64],
                                                                    in0=b.ps[po:po + 64, bo[ri], hh * 65: hh * 65 + 64],
                                                                    scalar1=rec[po:po + 64, hh:hh + 1], scalar2=None, op0=ALU.mult),
                             reads=[("ps", bo[ri]), ("REC", jp % 2, ri)], writes=[("yna", jp, ri)])
        b.barrier()
        if _NAS <= 5:
            return
        YNN = b.sb("YNN", [128, 4, L], BF16, R0 + 0)
        for j in range(16):
            b.op(b.act, lambda: nc.scalar.activation(out=junk[:], in_=yna[:, j, :], func=AF.Square),
                 reads=[("yna", j, 0), ("yna", j, 1)], writes=["junk"])
            b.op(b.dve, lambda: nc.vector.reduce_sum(out=ss[:, j:j + 1], in_=junk[:], axis=mybir.AxisListType.X),
                 reads=["junk"], writes=[("ss", j)])
        b.op(b.act, lambda: nc.scalar.activation(out=rs16[:], in_=ss[:], func=AF.Sqrt, bias=self.epsc[:, 0:1], scale=1.0 / 512.0),
             reads=[("ss", j) for j in range(16)] + ["epsc"], writes=["rs16"])
        b.op(b.dve, lambda: nc.vector.reciprocal(out=rs16[:], in_=rs16[:]), reads=[], writes=["rs16"])
        for j in range(16):
            yt = YT[j % 2]
            b.op(b.dve, lambda: nc.vector.tensor_scalar(out=yt[:], in0=yna[:, j, :], scalar1=rs16[:, j:j + 1], scalar2=None, op0=ALU.mult),
                 reads=[("yna", j, 0), ("yna", j, 1), "rs16"], writes=[("YT", j % 2)])
            bk = b.bank()

            def mmt():
                for c in range(4):
                    ins = nc.tensor.matmul(b.ps[:, bk, c * 128:(c + 1) * 128], yt[:, c * 128:(c + 1) * 128], self.ident[:], start=True, stop=True)
                return ins
            b.op(b.pe, mmt, reads=[("YT", j % 2), "ident"], writes=[("ps", bk)])
            for c in range(4):
                b.op(b.dve, lambda c=c: nc.vector.tensor_scalar(out=YNN[:, c, j * 128:(j + 1) * 128], in0=b.ps[:, bk, c * 128:(c + 1) * 128],
                                                                scalar1=self.gcol("g_out", l, c), scalar2=None, op0=ALU.mult),
                     reads=[("ps", bk), "cp"], writes=[("YNN", c, j // 4)])

    def ph_zero(self, which):
        b = self.b
        nc = b.nc
        if which == "na":
            t = b.sb("YNN", [128, 4, L], BF16, OFF_R + 0)
            b.op(b.dve, lambda: nc.vector.memset(t[:], 0.0), writes=[("YNN", c, n) for c in range(4) for n in range(NT)])
        else:
            t = b.sb("YNH", [128, 4, L], BF16, OFF_R + OFF_YNHY)
            b.op(b.dve, lambda: nc.vector.memset(t[:], 0.0), writes=[("YNH", c, n) for c in range(4) for n in range(NT)])

    def ph_wout(self, l):
        b = self.b
        nc = b.nc
        YNN = b.sb("YNN", [128, 4, L], BF16, OFF_R + 0)
        YNH = b.sb("YNH", [128, 4, L], BF16, OFF_R + OFF_YNHY)
        for d in range(NCH):
            (w,), wkey = self.W.get()
            for n in range(NT):
                ts = slice(n * TT, (n + 1) * TT)
                bk = b.bank()

                def mm():
                    for c in range(NCH):
                        src = YNN[:, c, ts] if c < 4 else YNH[:, c - 4, ts]
                        ins = nc.tensor.matmul(b.ps[:, bk, :], w[:, c, :], src, start=(c == 0), stop=(c == NCH - 1))
                    return ins
                b.op(b.pe, mm, reads=[("YNN", c, n) for c in range(4)] + [("YNH", c, n) for c in range(4)] + wkey, writes=[("ps", bk)])
                b.op(b.dve, lambda: nc.vector.tensor_tensor(out=self.X[:, d, ts], in0=self.X[:, d, ts], in1=b.ps[:, bk, :], op=ALU.add),
                     reads=[("ps", bk)], writes=[("X", d, n)])


def _run(inputs, launches, x_override=None):
    inp = {k: np.asarray(v) for k, v in inputs.items()}
    x = inp["x"] if x_override is None else x_override
    Bn = x.shape[0]
    xT = np.ascontiguousarray(np.transpose(x, (0, 2, 1)))
    pT = np.ascontiguousarray(np.transpose(inp["p"], (1, 0, 3, 2)))
    shared = {k: np.ascontiguousarray(v) for k, v in inp.items() if k not in ("x", "p")}
    shared["cpack"] = build_cpack(inp)
    shared["ident"] = np.eye(128, dtype=np.float32)
    shared["btab"] = build_bias_table(inp["na_rpb"])
    hc = host_constants()
    cur = xT
    for phases in launches:
        arrays0 = dict(shared)
        arrays0["xT"] = cur[0]
        arrays0["pT"] = pT[0]
        arrays0.update(hc)
        prog = Prog(None, phases, arrays0)
        nc = prog.build()
        names = list(prog.b.in_arrays.keys())
        in_maps = []
        for ci in range(Bn):
            m = {}
            for nm in names:
                if nm == "xT":
                    m[nm] = cur[ci]
                elif nm == "pT":
                    m[nm] = pT[ci]
                else:
                    m[nm] = prog.b.in_arrays[nm]
            in_maps.append(m)
        res = run_bass_kernel_spmd(nc, in_maps, core_ids=list(range(Bn)), trace=bool(os.environ.get('KTRACE')))
        if os.environ.get('KTRACE'):
            print('EXEC_NS', res.exec_time_ns)
        cur = np.stack([np.asarray(r["outT"]) for r in res.results], axis=0)
    return np.ascontiguousarray(np.transpose(cur, (0, 2, 1))).astype(np.float32)


def layer_phases(l):
    return [("filt", l), ("ffa", l), ("mixhy", l), ("mixna", l), ("wout", l), ("ffb", l), ("ple", l)]


def kernel(**inputs):
    phases = [("load",)]
    for l in range(DEPTH):
        phases += layer_phases(l)
    phases += [("store", True)]
    return _run(inputs, [phases])
```
